# Optimizing a Trainium2 kernel written in Bass

```python
import math
import jax, jax.numpy as jnp
from jax import lax
import numpy as np

D_MODEL = 2048
BATCH = 2
SEQ = 16384
DEPTH = 1

HEAD_DIM = 128
RET_HEADS = D_MODEL // (2 * HEAD_DIM)
ATT_HEADS = D_MODEL // (2 * HEAD_DIM)
RET_WIDTH = RET_HEADS * HEAD_DIM
ATT_WIDTH = ATT_HEADS * HEAD_DIM
MIX_WIDTH = RET_WIDTH + ATT_WIDTH
IN_WIDTH = 4 * RET_WIDTH + 3 * ATT_WIDTH
D_FF = 4 * D_MODEL
RET_CHUNK = 128
RET_DECAY_OFFSET_FWD = 0.0
RET_DECAY_OFFSET_BWD = 0.5
DILATED_PATTERNS = ((128, 1), (512, 4), (2048, 16))
ALPHA = (2.0 * DEPTH) ** 0.25
BETA = (8.0 * DEPTH) ** -0.25
LN_EPS = 1e-5
GN_EPS = 1e-5

kernel_name = 'hybrid_retention_dilated_attention_block'


def _layer_norm(x, gain, bias):
    xf = x.astype(jnp.float32)
    mu = jnp.mean(xf, axis=-1, keepdims=True)
    var = jnp.mean(jnp.square(xf - mu), axis=-1, keepdims=True)
    y = (xf - mu) * lax.rsqrt(var + LN_EPS) * gain.astype(jnp.float32) + bias.astype(jnp.float32)
    return y.astype(x.dtype)


def _split_heads(t, n_heads):
    b, s, _ = t.shape
    return t.reshape(b, s, n_heads, HEAD_DIM).transpose(0, 2, 1, 3).astype(jnp.float32)


def _merge_heads(t):
    b, h, s, d = t.shape
    return t.transpose(0, 2, 1, 3).reshape(b, s, h * d)


def _retention_log_decays(offset):
    h = jnp.arange(RET_HEADS, dtype=jnp.float32)
    return jnp.log1p(-jnp.exp2(-(5.0 + offset) - h))


def _retention_one_direction(q, k, v, log_gamma, strict):
    b, h, s, dh = q.shape
    c = RET_CHUNK
    nc = s // c
    qc = q.reshape(b, h, nc, c, dh)
    kc = k.reshape(b, h, nc, c, dh)
    vc = v.reshape(b, h, nc, c, dh)
    pos = jnp.arange(c, dtype=jnp.float32)
    diff = pos[:, None] - pos[None, :]
    keep = (diff > 0) if strict else (diff >= 0)
    lg = log_gamma[:, None, None]
    decay = jnp.where(keep[None], jnp.exp(jnp.maximum(diff, 0.0)[None] * lg), 0.0)
    scores = jnp.einsum('bhnqd,bhnkd->bhnqk', qc, kc) * decay[None, :, None]
    intra = jnp.einsum('bhnqk,bhnkd->bhnqd', scores, vc)
    k_decay = jnp.exp((c - 1.0 - pos)[None, :] * log_gamma[:, None])
    q_decay = jnp.exp((pos + 1.0)[None, :] * log_gamma[:, None])
    chunk_decay = jnp.exp(c * log_gamma)
    chunk_kv = jnp.einsum('bhnkd,bhnke->nbhde', kc * k_decay[None, :, None, :, None], vc)

    def step(state, kv):
        return state * chunk_decay[None, :, None, None] + kv, state

    _, prev_states = lax.scan(step, jnp.zeros((b, h, dh, dh), jnp.float32), chunk_kv)
    cross = jnp.einsum('bhnqd,nbhde->bhnqe', qc * q_decay[None, :, None, :, None], prev_states)
    return (intra + cross).reshape(b, h, s, dh)


def _retention_group(q, k, v, g, gn_gain):
    qh = _split_heads(q, RET_HEADS)
    kh = _split_heads(k, RET_HEADS) * (HEAD_DIM ** -0.5)
    vh = _split_heads(v, RET_HEADS)
    fwd = _retention_one_direction(qh, kh, vh, _retention_log_decays(RET_DECAY_OFFSET_FWD), False)
    flip = lambda t: jnp.flip(t, axis=2)
    bwd = flip(_retention_one_direction(flip(qh), flip(kh), flip(vh),
                                        _retention_log_decays(RET_DECAY_OFFSET_BWD), True))
    r = fwd + bwd
    mu = jnp.mean(r, axis=-1, keepdims=True)
    var = jnp.mean(jnp.square(r - mu), axis=-1, keepdims=True)
    r = (r - mu) * lax.rsqrt(var + GN_EPS)
    r = _merge_heads(r) * gn_gain.astype(jnp.float32)
    return (jax.nn.silu(g.astype(jnp.float32)) * r).astype(q.dtype)


def _alibi_slopes(n_heads):
    return jnp.exp2(-8.0 * (jnp.arange(n_heads, dtype=jnp.float32) + 1.0) / n_heads)


def _dilated_branch(q, k, v, slopes, window, dilation):
    b, h, s, dh = q.shape
    w = window // (2 * dilation)
    ln = s // dilation
    nb = -(-ln // w)
    lp = nb * w

    def to_strided(t):
        return t.reshape(b, h, ln, dilation, dh).transpose(0, 1, 3, 2, 4)

    qs = jnp.pad(to_strided(q), ((0, 0), (0, 0), (0, 0), (0, lp - ln), (0, 0)))
    qs = qs.reshape(b, h, dilation, nb, w, dh)
    pad_kv = ((0, 0), (0, 0), (0, 0), (w, lp - ln + w), (0, 0))
    kp = jnp.pad(to_strided(k), pad_kv)
    vp = jnp.pad(to_strided(v), pad_kv)

    def band_blocks(t):
        return jnp.concatenate(
            [t[:, :, :, o * w:o * w + lp].reshape(b, h, dilation, nb, w, dh) for o in range(3)],
            axis=-2)

    kb = band_blocks(kp)
    vb = band_blocks(vp)
    q_idx = jnp.arange(lp).reshape(nb, w)
    k_idx = jnp.arange(nb)[:, None] * w - w + jnp.arange(3 * w)[None, :]
    rel = k_idx[:, None, :] - q_idx[:, :, None]
    valid = (jnp.abs(rel) <= w) & (k_idx[:, None, :] >= 0) & (k_idx[:, None, :] < ln)
    dist = (jnp.abs(rel) * dilation).astype(jnp.float32)
    bias = -slopes[:, None, None, None] * dist[None]
    sc = jnp.einsum('bhrnqd,bhrnkd->bhrnqk', qs, kb) + bias[None, :, None]
    sc = jnp.where(valid, sc, -jnp.inf)
    m = jnp.max(sc, axis=-1, keepdims=True)
    p = jnp.exp(sc - m)
    den = jnp.sum(p, axis=-1)
    o = jnp.einsum('bhrnqk,bhrnkd->bhrnqd', p, vb) / den[..., None]
    lse = m[..., 0] + jnp.log(den)
    o = o.reshape(b, h, dilation, lp, dh)[:, :, :, :ln].transpose(0, 1, 3, 2, 4).reshape(b, h, s, dh)
    lse = lse.reshape(b, h, dilation, lp)[:, :, :, :ln].transpose(0, 1, 3, 2).reshape(b, h, s)
    return o, lse


def _dilated_attention_group(q, k, v):
    qh = _split_heads(q, ATT_HEADS) * (HEAD_DIM ** -0.5)
    kh = _split_heads(k, ATT_HEADS)
    vh = _split_heads(v, ATT_HEADS)
    slopes = _alibi_slopes(ATT_HEADS)
    outs, lses = [], []
    for window, dilation in DILATED_PATTERNS:
        o, l = _dilated_branch(qh, kh, vh, slopes, window, dilation)
        outs.append(o)
        lses.append(l)
    weights = jax.nn.softmax(jnp.stack(lses, axis=0), axis=0)
    o = jnp.sum(weights[..., None] * jnp.stack(outs, axis=0), axis=0)
    return _merge_heads(o).astype(q.dtype)


def _hybrid_mixer(x, w_in, ret_gn_gain, w_out):
    proj = x @ w_in
    splits = np.cumsum([RET_WIDTH] * 4 + [ATT_WIDTH] * 2).tolist()
    rq, rk, rv, rg, aq, ak, av = jnp.split(proj, splits, axis=-1)
    ret = _retention_group(rq, rk, rv, rg, ret_gn_gain)
    att = _dilated_attention_group(aq, ak, av)
    return jnp.concatenate([ret, att], axis=-1) @ w_out


def _squared_relu_mlp(x, w_up, w_down):
    return jnp.square(jax.nn.relu(x @ w_up)) @ w_down


def setup_inputs(seed: int = 0) -> dict:
    key = jax.random.key(seed)
    ks = jax.random.split(key, 10)
    nrm = jax.random.normal
    x = nrm(ks[0], (BATCH, SEQ, D_MODEL), jnp.float32)
    w_in = nrm(ks[1], (DEPTH, D_MODEL, IN_WIDTH), jnp.float32) * D_MODEL ** -0.5
    ret_gn_gain = 1.0 + 0.02 * nrm(ks[2], (DEPTH, RET_WIDTH), jnp.float32)
    w_out = nrm(ks[3], (DEPTH, MIX_WIDTH, D_MODEL), jnp.float32) * (MIX_WIDTH ** -0.5 * BETA)
    ln1_gain = 1.0 + 0.02 * nrm(ks[4], (DEPTH, D_MODEL), jnp.float32)
    ln1_bias = 0.02 * nrm(ks[5], (DEPTH, D_MODEL), jnp.float32)
    w_up = nrm(ks[6], (DEPTH, D_MODEL, D_FF), jnp.float32) * D_MODEL ** -0.5
    w_down = nrm(ks[7], (DEPTH, D_FF, D_MODEL), jnp.float32) * (D_FF ** -0.5 * BETA)
    ln2_gain = 1.0 + 0.02 * nrm(ks[8], (DEPTH, D_MODEL), jnp.float32)
    ln2_bias = 0.02 * nrm(ks[9], (DEPTH, D_MODEL), jnp.float32)
    return {'x': x, 'w_in': w_in, 'ret_gn_gain': ret_gn_gain, 'w_out': w_out,
            'ln1_gain': ln1_gain, 'ln1_bias': ln1_bias, 'w_up': w_up, 'w_down': w_down,
            'ln2_gain': ln2_gain, 'ln2_bias': ln2_bias}


def reference(x, w_in, ret_gn_gain, w_out, ln1_gain, ln1_bias, w_up, w_down, ln2_gain, ln2_bias):
    h = x
    for layer in range(DEPTH):
        mix = _hybrid_mixer(h, w_in[layer], ret_gn_gain[layer], w_out[layer])
        h = _layer_norm(ALPHA * h + mix, ln1_gain[layer], ln1_bias[layer])
        ffn = _squared_relu_mlp(h, w_up[layer], w_down[layer])
        h = _layer_norm(ALPHA * h + ffn, ln2_gain[layer], ln2_bias[layer])
    return h
```

```python
import contextlib
import math
import numpy as np
import concourse.bass as bass
import concourse.mybir as mybir
from concourse.bass_utils import run_bass_kernel_spmd

F32 = mybir.dt.float32
BF16 = mybir.dt.bfloat16
AF = mybir.ActivationFunctionType
ALU = mybir.AluOpType
AX = mybir.AxisListType

S_LEN = 16384
D = 2048
NTOK = 4096
ALPHA = 2.0 ** 0.25
EPS = 1e-5
SC = 128.0 ** -0.5
ENGS = ("pe", "act", "dve", "pool", "sp")


class Tok:
    __slots__ = ("w", "r")

    def __init__(self):
        self.w = None
        self.r = {}


class Op:
    __slots__ = ("eng", "fn", "deps", "key", "idx", "milestone", "count", "dsem")

    def __init__(self, eng, fn, key, idx, dsem):
        self.eng, self.fn, self.key, self.idx, self.dsem = eng, fn, key, idx, dsem
        self.deps = {}
        self.milestone = False
        self.count = None


_PROG_ID = [0]


class Prog:
    def __init__(self, nc):
        self.nc = nc
        _PROG_ID[0] += 1
        self.pid = _PROG_ID[0]
        self.ops = {e: [] for e in ENGS}
        self.keyidx = {}
        self.last = {}

    def op(self, eng, fn, reads=(), writes=(), dsem=None):
        key = eng if dsem is None else ("dma", dsem)
        idx = self.keyidx.get(key, 0)
        self.keyidx[key] = idx + 1
        o = Op(eng, fn, key, idx, dsem)
        deps = o.deps

        def need(p):
            if p is None:
                return
            if p.key == "pe" and key == "pe":
                return
            q = deps.get(p.key)
            if q is None or q.idx < p.idx:
                deps[p.key] = p

        for t in reads:
            need(t.w)
        for t in writes:
            need(t.w)
            for p in t.r.values():
                need(p)
        for t in reads:
            t.r[key] = o
        for t in writes:
            t.w = o
            t.r = {}
        self.ops[eng].append(o)
        self.last[key] = o
        return o

    def emit(self):
        nc = self.nc
        finals = [o for k, o in self.last.items() if isinstance(k, tuple)]
        for e in ENGS:
            for o in self.ops[e]:
                for p in o.deps.values():
                    p.milestone = True
        cnt = {}
        for e in ENGS:
            for o in self.ops[e]:
                if o.dsem is not None:
                    cnt[o.key] = cnt.get(o.key, 0) + 16
                    o.count = cnt[o.key]
                elif o.milestone:
                    cnt[o.key] = cnt.get(o.key, 0) + 1
                    o.count = cnt[o.key]
        with contextlib.ExitStack() as stack:
            sems = {}
            for e in ENGS:
                sems[e] = stack.enter_context(nc.semaphore(f"s{self.pid}_{e}"))
            for k in self.keyidx:
                if isinstance(k, tuple):
                    sems[k] = stack.enter_context(nc.semaphore(f"d{self.pid}_{k[1]}"))
            block = stack.enter_context(nc.Block(f"blk{self.pid}"))
            prog = self

            def run(e, eng):
                waited = {}
                for o in prog.ops[e]:
                    for k, p in o.deps.items():
                        if waited.get(k, 0) >= p.count:
                            continue
                        eng.wait_ge(sems[k], p.count)
                        waited[k] = p.count
                    ins = o.fn(eng)
                    if o.dsem is not None:
                        ins.then_inc(sems[o.key], 16)
                    elif o.milestone:
                        ins.then_inc(sems[e], 1)
                if e == "sp":
                    for p in finals:
                        if waited.get(p.key, 0) < p.count:
                            eng.wait_ge(sems[p.key], p.count)

            block.tensor(lambda eng: run("pe", eng))
            block.scalar(lambda eng: run("act", eng))
            block.vector(lambda eng: run("dve", eng))
            block.gpsimd(lambda eng: run("pool", eng))
            block.sync(lambda eng: run("sp", eng))


def bc(ap, shape):
    return ap.unsqueeze(2).to_broadcast(shape)


def mk_ident(P, nc, st, name):
    idf = st.enter_context(nc.sbuf_tensor(name + "f", [128, 128], F32))
    idb = st.enter_context(nc.sbuf_tensor(name, [128, 128], BF16))
    t = Tok()
    P.op("pool", lambda e: e.memset(idf[:, :], 0.0), writes=[t])
    P.op("pool", lambda e: e.affine_select(out=idf[:, :], in_=idf[:, :], pattern=[[-1, 128]],
                                           compare_op=ALU.not_equal, fill=1.0, base=0,
                                           channel_multiplier=1), reads=[t], writes=[t])
    P.op("dve", lambda e: e.tensor_copy(idb[:, :], idf[:, :]), reads=[t], writes=[t])
    return idb, t


def kview(ap2d):
    return ap2d.rearrange("(k p) n -> p k n", p=128)


def phase0(nc, Dm):
    with contextlib.ExitStack() as st:
        P = Prog(nc)
        T = lambda n, s, d: st.enter_context(nc.sbuf_tensor(n, s, d))
        PS = lambda n, s, d=F32: st.enter_context(nc.psum_tensor(n, s, d))
        wkv = T("p0_wkv", [128, 16, 2048], BF16)
        xt = [T(f"p0_xt{i}", [128, 16, 512], BF16) for i in range(2)]
        ksb = [T(f"p0_k{i}", [128, 1024], BF16) for i in range(2)]
        vf = [T(f"p0_vf{i}", [128, 8, 128], BF16) for i in range(2)]
        vb = [T(f"p0_vb{i}", [128, 8, 128], BF16) for i in range(2)]
        Sf = T("p0_Sf", [128, 8, 128], F32)
        Sb = T("p0_Sb", [128, 8, 128], F32)
        rs_oth = T("p0_rs", [128, 96, 16], F32)
        af_oth = T("p0_af", [128, 96, 8], F32)
        otab = T("p0_ot", [128, 48], F32)
        kp = PS("p0_kp", [128, 1024])
        vp = PS("p0_vp", [128, 1024])
        kvf = PS("p0_kvf", [128, 8, 128])
        kvb = PS("p0_kvb", [128, 8, 128])
        t_w = [Tok() for _ in range(4)]
        t_x = [Tok(), Tok()]
        t_ks = [Tok(), Tok()]
        t_vf = [Tok(), Tok()]
        t_vb = [Tok(), Tok()]
        t_Sf, t_Sb, t_tab = Tok(), Tok(), Tok()
        t_kp, t_vp, t_kvf, t_kvb = Tok(), Tok(), Tok(), Tok()
        win = kview(Dm["w_in"])
        for g in range(4):
            P.op("pool", lambda e, g=g: e.dma_start(out=wkv[:, 4 * g:4 * g + 4, :],
                                                  in_=win[:, 4 * g:4 * g + 4, 1024:3072]),
                 writes=[t_w[g]], dsem=f"w{g}")
        P.op("sp", lambda e: e.dma_start(out=rs_oth[:, :, :], in_=Dm["rs_oth"]), writes=[t_tab], dsem="tab")
        P.op("sp", lambda e: e.dma_start(out=af_oth[:, :, :], in_=Dm["af_oth"]), writes=[t_tab], dsem="tab")
        P.op("sp", lambda e: e.dma_start(out=otab[:, :], in_=Dm["own_tab"]), writes=[t_tab], dsem="tab")
        P.op("dve", lambda e: e.memset(Sf[:, :, :], 0.0), writes=[t_Sf])
        P.op("dve", lambda e: e.memset(Sb[:, :, :], 0.0), writes=[t_Sb])

        conv = []
        for i in range(4):
            conv.append((Dm["w_out_b"][512 * i:512 * i + 512, :], Dm["w_out"][512 * i:512 * i + 512, :]))
        for i in range(16):
            conv.append((Dm["w_up_b"][128 * i:128 * i + 128, :], Dm["w_up"][128 * i:128 * i + 128, :]))
        for i in range(16):
            conv.append((Dm["w_down_b"][512 * i:512 * i + 512, :], Dm["w_down"][512 * i:512 * i + 512, :]))

        xo = kview(Dm["xT_oth"])
        xw = kview(Dm["xT_own"])
        steps = []
        for tl in range(24):
            for cc in range(4):
                steps.append(dict(src=xo, col=tl * 512, cc=cc, own=False, ts=tl * 4 + cc, first=(cc == 0)))
        for tau in range(7, -1, -1):
            for cc in range(3, -1, -1):
                steps.append(dict(src=xw, col=(tau + 2) * 512, cc=cc, own=True, ts=None, first=(cc == 3), tau=tau))
        nst = len(steps)
        tile_no = [0]

        def emit_proj(n):
            s = steps[n]
            if s["first"]:
                tile_no[0] += 1
            xs = tile_no[0] % 2
            s["xs"] = xs
            if s["first"]:
                P.op("pool", lambda e: e.dma_start(out=xt[xs][:, :, :], in_=s["src"][:, :, s["col"]:s["col"] + 512]),
                     writes=[t_x[xs]], dsem=f"x{xs}")
                if conv:
                    dst, srcw = conv.pop(0)
                    P.op("pool", lambda e: e.dma_start(out=dst, in_=srcw), dsem="conv")
            b = n % 2
            c0 = s["cc"] * 128
            for half in range(2):
                for kc in range(16):
                    P.op("pe", lambda e, kc=kc, half=half: e.matmul(
                        kp[:, half * 512:half * 512 + 512], xt[xs][:, kc, c0:c0 + 128],
                        wkv[:, kc, half * 512:half * 512 + 512], start=(kc == 0), stop=(kc == 15)),
                        reads=[t_x[xs], t_w[kc // 4]], writes=[t_kp])
            P.op("act", lambda e: e.activation(out=ksb[b][:, :], in_=kp[:, :], func=AF.Copy),
                 reads=[t_kp], writes=[t_ks[b]])
            for half in range(2):
                for kc in range(16):
                    P.op("pe", lambda e, kc=kc, half=half: e.matmul(
                        vp[:, half * 512:half * 512 + 512], xt[xs][:, kc, c0:c0 + 128],
                        wkv[:, kc, 1024 + half * 512:1024 + half * 512 + 512], start=(kc == 0), stop=(kc == 15)),
                        reads=[t_x[xs], t_w[kc // 4]], writes=[t_vp])
            vp3 = vp[:, :].rearrange("p (h e) -> p h e", e=128)
            if s["own"]:
                ci = s["tau"] * 4 + s["cc"]
                P.op("dve", lambda e: e.tensor_tensor(out=vb[b][:, :, :], in0=vp3, in1=bc(otab[:, 8:16], [128, 8, 128]),
                                                      op=ALU.mult), reads=[t_vp, t_tab], writes=[t_vb[b]])
                P.op("dve", lambda e: e.tensor_tensor(out=vf[b][:, :, :], in0=vp3, in1=bc(otab[:, 0:8], [128, 8, 128]),
                                                      op=ALU.mult), reads=[t_vp, t_tab], writes=[t_vf[b]])
                P.op("sp", lambda e: e.dma_start(out=Dm["own_k"][ci], in_=ksb[b][:, :]), reads=[t_ks[b]], dsem=f"ks{b}")
                P.op("sp", lambda e: e.dma_start(out=Dm["own_vf"][ci], in_=vf[b][:, :, :].rearrange("p h e -> p (h e)")),
                     reads=[t_vf[b]], dsem=f"vfs{b}")
                P.op("sp", lambda e: e.dma_start(out=Dm["own_vb"][ci], in_=vb[b][:, :, :].rearrange("p h e -> p (h e)")),
                     reads=[t_vb[b]], dsem=f"vbs{b}")
            else:
                ts = s["ts"]
                P.op("dve", lambda e: e.tensor_tensor(out=vf[b][:, :, :], in0=vp3,
                                                      in1=bc(rs_oth[:, ts, 0:8], [128, 8, 128]), op=ALU.mult),
                     reads=[t_vp, t_tab], writes=[t_vf[b]])
                P.op("dve", lambda e: e.tensor_tensor(out=vb[b][:, :, :], in0=vp3,
                                                      in1=bc(rs_oth[:, ts, 8:16], [128, 8, 128]), op=ALU.mult),
                     reads=[t_vp, t_tab], writes=[t_vb[b]])

        def emit_kv(n):
            s = steps[n]
            b = n % 2
            if not s["own"]:
                ts = s["ts"]
                for h in range(8):
                    P.op("pe", lambda e, h=h: e.matmul(kvf[:, h, :], ksb[b][:, h * 128:h * 128 + 128], vf[b][:, h, :],
                                                      start=True, stop=True),
                         reads=[t_ks[b], t_vf[b]], writes=[t_kvf])
                P.op("dve", lambda e: e.tensor_tensor(out=Sf[:, :, :], in0=Sf[:, :, :],
                                                      in1=bc(af_oth[:, ts, :], [128, 8, 128]), op=ALU.mult),
                     reads=[t_tab], writes=[t_Sf])
                P.op("dve", lambda e: e.tensor_tensor(out=Sf[:, :, :], in0=Sf[:, :, :], in1=kvf[:, :, :], op=ALU.add),
                     reads=[t_kvf], writes=[t_Sf])
            for h in range(8):
                P.op("pe", lambda e, h=h: e.matmul(kvb[:, h, :], ksb[b][:, h * 128:h * 128 + 128], vb[b][:, h, :],
                                                  start=True, stop=True),
                     reads=[t_ks[b], t_vb[b]], writes=[t_kvb])
            P.op("dve", lambda e: e.tensor_tensor(out=Sb[:, :, :], in0=Sb[:, :, :],
                                                  in1=bc(otab[:, 40:48], [128, 8, 128]), op=ALU.mult),
                 reads=[t_tab], writes=[t_Sb])
            P.op("dve", lambda e: e.tensor_tensor(out=Sb[:, :, :], in0=Sb[:, :, :], in1=kvb[:, :, :], op=ALU.add),
                 reads=[t_kvb], writes=[t_Sb])
            if n + 1 < nst and steps[n + 1]["own"] and steps[n + 1]["first"]:
                tau = steps[n + 1]["tau"]
                P.op("sp", lambda e: e.dma_start(out=Dm["sb_bound"][tau], in_=Sb[:, :, :]), reads=[t_Sb], dsem="sbst")
                if not s["own"]:
                    P.op("sp", lambda e: e.dma_start(out=Dm["sf_in"], in_=Sf[:, :, :]), reads=[t_Sf], dsem="sfst")

        for n in range(nst):
            emit_proj(n)
            if n > 0:
                emit_kv(n - 1)
        emit_kv(nst - 1)
        while conv:
            dst, srcw = conv.pop(0)
            P.op("pool", lambda e, dst=dst, srcw=srcw: e.dma_start(out=dst, in_=srcw), dsem="conv")
        P.emit()


def phase1r(nc, Dm):
    with contextlib.ExitStack() as st:
        P = Prog(nc)
        T = lambda n, s, d: st.enter_context(nc.sbuf_tensor(n, s, d))
        PS = lambda n, s, d=F32: st.enter_context(nc.psum_tensor(n, s, d))
        wr = T("r_w", [128, 16, 1024], BF16)
        xt = [T(f"r_xt{i}", [128, 16, 512], BF16) for i in range(2)]
        qT = [T(f"r_qT{i}", [128, 4, 512], BF16) for i in range(2)]
        kT = [T(f"r_kT{i}", [128, 4, 512], BF16) for i in range(2)]
        ktok = [T(f"r_kt{i}", [128, 4, 512], BF16) for i in range(2)]
        vf = [T(f"r_vf{i}", [128, 4, 4, 128], BF16) for i in range(2)]
        vb = [T(f"r_vb{i}", [128, 4, 4, 128], BF16) for i in range(2)]
        sg = [T(f"r_sg{i}", [128, 4, 512], F32) for i in range(2)]
        S = [T("r_Sf", [128, 4, 128], F32), T("r_Sb0", [128, 4, 128], F32), T("r_Sb1", [128, 4, 128], F32)]
        W = [T("r_Wf", [128, 4, 128], BF16), T("r_Wb", [128, 4, 128], BF16)]
        Pm = [T(f"r_P{i}", [128, 4, 128], BF16) for i in range(2)]
        r = T("r_r", [128, 16, 128], F32)
        tmp = T("r_tmp", [128, 16, 128], F32)
        sq = T("r_sq", [128, 16, 128], F32)
        rtk = T("r_rtk", [128, 4, 512], BF16)
        retT = [T(f"r_retT{i}", [128, 4, 512], BF16) for i in range(2)]
        otab = T("r_ot", [128, 48], F32)
        msk = T("r_msk", [128, 2, 4, 128], F32)
        gng = T("r_gng", [128, 1024], F32)
        stt = T("r_st", [128, 6, 16], F32)
        ident, t_id = mk_ident(P, nc, st, "r_id")
        pq = [PS(f"r_pq{i}", [128, 512]) for i in range(2)]
        sc = [PS(f"r_sc{i}", [128, 4, 128]) for i in range(2)]
        ob = [PS(f"r_o{i}", [128, 4, 128]) for i in range(2)]
        kv = PS("r_kv", [128, 4, 128])
        tr = PS("r_tr", [128, 4, 128], BF16)
        t_w = [Tok() for _ in range(2)]
        t_x = [Tok(), Tok()]
        t_q, t_k, t_sg = [Tok(), Tok()], [Tok(), Tok()], [Tok(), Tok()]
        t_kt, t_vf, t_vb = [Tok(), Tok()], [Tok(), Tok()], [Tok(), Tok()]
        t_S = [Tok(), Tok(), Tok()]
        t_W = [Tok(), Tok()]
        t_P = [Tok(), Tok()]
        t_r = [Tok() for _ in range(4)]
        t_tmp = [Tok() for _ in range(4)]
        t_sq, t_rtk, t_st, t_tab = Tok(), Tok(), Tok(), Tok()
        t_rT = [Tok(), Tok()]
        t_pq = [Tok() for _ in range(2)]
        t_sc, t_o, t_kv, t_tr = [Tok(), Tok()], [Tok(), Tok()], Tok(), Tok()
        P.op("sp", lambda e: e.dma_start(out=otab[:, :], in_=Dm["own_tab"]), writes=[t_tab], dsem="tab")
        P.op("sp", lambda e: e.dma_start(out=msk[:, :, :, :], in_=Dm["rmask"]), writes=[t_tab], dsem="tab")
        P.op("sp", lambda e: e.dma_start(out=gng[:, :], in_=Dm["gng"]), writes=[t_tab], dsem="tab")
        win = kview(Dm["w_in"])
        xw = kview(Dm["xT_own"])
        mixv = kview(Dm["mixT"])
        pqi = [0]
        pi = [0]
        tp = [0]
        prev = [None]
        L = (wr, xt, qT, kT, ktok, vf, vb, sg, S, W, Pm, r, tmp, sq, rtk, retT, otab, msk, gng, stt, ident, pq, sc, ob, kv, tr,
             t_w, t_x, t_q, t_k, t_kt, t_vf, t_vb, t_sg, t_S, t_W, t_P, t_r, t_tmp, t_sq, t_rtk, t_st, t_tab, t_rT, t_pq,
             t_sc, t_o, t_kv, t_tr, t_id, xw, mixv, pqi, pi)
        loads, units, front_b, back_gn, back_tr = _phase1r_tile_factory(P, nc, Dm, L)

        for hg in range(2):
            for wi, seg in enumerate((0, 3)):
                c0 = seg * 1024 + hg * 512
                P.op("pool", lambda e, wi=wi, c0=c0: e.dma_start(out=wr[:, :, wi * 512:wi * 512 + 512],
                                                               in_=win[:, :, c0:c0 + 512]),
                     writes=[t_w[wi]], dsem=f"w{wi}")
            loads(hg, 0, tp[0] % 2)
            for u_ in units(hg, 0, tp[0] % 2):
                u_()
            for j in range(8):
                par = tp[0] % 2
                extra = []
                if j + 1 < 8:
                    loads(hg, j + 1, 1 - par)
                    extra = units(hg, j + 1, 1 - par)
                mid = None
                if prev[0] is not None:
                    pv = prev[0]
                    mid = lambda pv=pv: back_tr(*pv)
                front_b(hg, j, par, extra, mid)
                back_gn(hg, j, par)
                prev[0] = (hg, j, par)
                tp[0] += 1
        back_tr(*prev[0])
        P.emit()


def _phase1r_tile_factory(P, nc, Dm, L):
    (wr, xt, qT, kT, ktok, vf, vb, sg, S, W, Pm, r, tmp, sq, rtk, retT, otab, msk, gng, stt, ident, pq, sc, ob, kv, tr,
     t_w, t_x, t_q, t_k, t_kt, t_vf, t_vb, t_sg, t_S, t_W, t_P, t_r, t_tmp, t_sq, t_rtk, t_st, t_tab, t_rT, t_pq,
     t_sc, t_o, t_kv, t_tr, t_id, xw, mixv, pqi, pi) = L

    def proj(lhsT_fn, rhs_fn, reads, evac):
        i = pqi[0] % 2
        pqi[0] += 1
        for kc in range(16):
            P.op("pe", lambda e, kc=kc: e.matmul(pq[i][:, :], lhsT_fn(kc), rhs_fn(kc), start=(kc == 0), stop=(kc == 15)),
                 reads=reads, writes=[t_pq[i]])
        evac(pq[i], t_pq[i])

    def loads(hg, j, xs):
                P.op("pool", lambda e: e.dma_start(out=xt[xs][:, :, :], in_=xw[:, :, (j + 2) * 512:(j + 3) * 512]),
                     writes=[t_x[xs]], dsem=f"x{xs}")
                if j == 0:
                    P.op("sp", lambda e: e.dma_start(out=S[0][:, :, :], in_=Dm["sf_in"][:, hg * 4:hg * 4 + 4, :]),
                         writes=[t_S[0]], dsem="sf")
                P.op("sp", lambda e: e.dma_start(out=S[1 + xs][:, :, :], in_=Dm["sb_bound"][j][:, hg * 4:hg * 4 + 4, :]),
                     writes=[t_S[1 + xs]], dsem=f"sb{xs}")
                for nm, dst, tk in (("own_k", ktok[xs][:, :, :], t_kt[xs]),
                                    ("own_vf", vf[xs][:, :, :, :].rearrange("p c h e -> p c (h e)"), t_vf[xs]),
                                    ("own_vb", vb[xs][:, :, :, :].rearrange("p c h e -> p c (h e)"), t_vb[xs])):
                    src = Dm[nm][j * 4:(j + 1) * 4].rearrange("c p n -> p c n")[:, :, hg * 512:(hg + 1) * 512]
                    P.op("sp", lambda e, dst=dst, src=src: e.dma_start(out=dst, in_=src), writes=[tk], dsem=f"{nm}{xs}")

    def units(hg, j, xs):
                us = []
                for h in range(4):
                    us.append(lambda h=h: proj(
                        lambda kc: wr[:, kc, h * 128:h * 128 + 128], lambda kc: xt[xs][:, kc, :], [t_w[0], t_x[xs]],
                        lambda ps, tps: P.op("act", lambda e: e.activation(out=qT[xs][:, h, :], in_=ps[:, :], func=AF.Copy),
                                             reads=[tps], writes=[t_q[xs]])))

                def kt_unit(h):
                    for cc in range(4):
                        P.op("pe", lambda e, cc=cc: e.transpose(tr[:, cc, :], ktok[xs][:, cc, h * 128:h * 128 + 128], ident[:, :]),
                             reads=[t_kt[xs], t_id], writes=[t_tr])
                    P.op("act", lambda e: e.activation(out=kT[xs][:, h, :].rearrange("p (c t) -> p c t", t=128),
                                                       in_=tr[:, :, :], func=AF.Copy),
                         reads=[t_tr], writes=[t_k[xs]])
                for h in range(4):
                    us.append(lambda h=h: kt_unit(h))
                for cc in range(4):
                    us.append(lambda cc=cc: proj(
                        lambda kc: xt[xs][:, kc, cc * 128:cc * 128 + 128], lambda kc: wr[:, kc, 512:1024], [t_w[1], t_x[xs]],
                        lambda ps, tps: P.op("act", lambda e: e.activation(out=sg[xs][:, cc, :], in_=ps[:, :], func=AF.Silu),
                                             reads=[tps], writes=[t_sg[xs]])))
                return us

    def front_b(hg, j, par, extra, mid):
                r_done = [False] * 4
                kt_, tkt_ = ktok[par], t_kt[par]
                qT_, kT_, tq_, tk_ = qT[par], kT[par], t_q[par], t_k[par]
                Sd = [S[0], S[1 + par]]
                tSd = [t_S[0], t_S[1 + par]]
                nstep = [0]
                pending = []

                def chunk(cc, d):
                    vt, tvt = (vf[par], t_vf[par]) if d == 0 else (vb[par], t_vb[par])
                    gcol = 32 + 8 * d + hg * 4
                    ocol = 16 + 8 * d + hg * 4
                    p_i = pi[0] % 2
                    pi[0] += 1
                    P.op("dve", lambda e: e.tensor_tensor(out=W[d][:, :, :], in0=Sd[d][:, :, :],
                                                          in1=bc(otab[:, gcol:gcol + 4], [128, 4, 128]), op=ALU.mult),
                         reads=[tSd[d], t_tab], writes=[t_W[d]])
                    for h in range(4):
                        P.op("pe", lambda e, h=h: e.matmul(sc[d][:, h, :], kT_[:, h, cc * 128:cc * 128 + 128],
                                                          qT_[:, h, cc * 128:cc * 128 + 128], start=True, stop=True),
                             reads=[tk_, tq_], writes=[t_sc[d]])
                    P.op("dve", lambda e: e.tensor_tensor(out=Pm[p_i][:, :, :], in0=sc[d][:, :, :], in1=msk[:, d, :, :], op=ALU.mult),
                         reads=[t_sc[d], t_tab], writes=[t_P[p_i]])
                    for h in range(4):
                        P.op("pe", lambda e, h=h: e.matmul(kv[:, h, :], kt_[:, cc, h * 128:h * 128 + 128], vt[:, cc, h, :],
                                                          start=True, stop=True),
                             reads=[tkt_, tvt], writes=[t_kv])
                    P.op("dve", lambda e: e.tensor_tensor(out=Sd[d][:, :, :], in0=Sd[d][:, :, :],
                                                          in1=bc(otab[:, gcol:gcol + 4], [128, 4, 128]), op=ALU.mult),
                         reads=[t_tab], writes=[tSd[d]])
                    P.op("dve", lambda e: e.tensor_tensor(out=Sd[d][:, :, :], in0=Sd[d][:, :, :], in1=kv[:, :, :], op=ALU.add),
                         reads=[t_kv], writes=[tSd[d]])
                    if pending:
                        pending.pop(0)()
                    for _ in range(2 if nstep[0] < 4 else 1):
                        if extra:
                            extra.pop(0)()
                    if nstep[0] == 3 and mid is not None:
                        mid()
                    nstep[0] += 1
                    for h in range(4):
                        P.op("pe", lambda e, h=h: e.matmul(ob[d][:, h, :], Pm[p_i][:, h, :], vt[:, cc, h, :],
                                                          start=(h == 0), stop=False, skip_group_check=True),
                             reads=[t_P[p_i], tvt], writes=[t_o[d]])
                        P.op("pe", lambda e, h=h: e.matmul(ob[d][:, h, :], qT_[:, h, cc * 128:cc * 128 + 128], W[d][:, h, :],
                                                          start=False, stop=True, skip_group_check=True),
                             reads=[tq_, t_W[d]], writes=[t_o[d]])
                    rv = r[:, cc * 4:cc * 4 + 4, :]
                    first = not r_done[cc]
                    r_done[cc] = True

                    def evac():
                        if first:
                            P.op("dve", lambda e: e.tensor_tensor(out=rv, in0=ob[d][:, :, :], in1=bc(otab[:, ocol:ocol + 4], [128, 4, 128]),
                                                                  op=ALU.mult), reads=[t_o[d], t_tab], writes=[t_r[cc]])
                        else:
                            tv = tmp[:, cc * 4:cc * 4 + 4, :]
                            P.op("dve", lambda e: e.tensor_tensor(out=tv, in0=ob[d][:, :, :], in1=bc(otab[:, ocol:ocol + 4], [128, 4, 128]),
                                                                  op=ALU.mult), reads=[t_o[d], t_tab], writes=[t_tmp[cc]])
                            P.op("pool", lambda e: e.tensor_tensor(out=rv, in0=rv, in1=tv, op=ALU.add),
                                 reads=[t_tmp[cc]], writes=[t_r[cc]])
                    pending.append(evac)

                for cc in range(4):
                    chunk(cc, 0)
                    chunk(3 - cc, 1)
                while pending:
                    pending.pop(0)()
                while extra:
                    extra.pop(0)()
    def back_gn(hg, j, sgi):
                allr = list(t_r)
                P.op("dve", lambda e: e.tensor_reduce(out=stt[:, 0, :], in_=r[:, :, :], axis=AX.X, op=ALU.add),
                     reads=allr, writes=[t_st])
                P.op("pool", lambda e: e.tensor_tensor(out=sq[:, :, :], in0=r[:, :, :], in1=r[:, :, :], op=ALU.mult),
                     reads=allr, writes=[t_sq])
                P.op("dve", lambda e: e.tensor_reduce(out=stt[:, 1, :], in_=sq[:, :, :], axis=AX.X, op=ALU.add),
                     reads=[t_sq], writes=[t_st])
                P.op("dve", lambda e: e.tensor_scalar(out=stt[:, 2, :], in0=stt[:, 0, :], scalar1=1.0 / 128, scalar2=None, op0=ALU.mult),
                     reads=[t_st], writes=[t_st])
                P.op("dve", lambda e: e.tensor_tensor(out=stt[:, 3, :], in0=stt[:, 2, :], in1=stt[:, 2, :], op=ALU.mult),
                     reads=[t_st], writes=[t_st])
                P.op("dve", lambda e: e.scalar_tensor_tensor(out=stt[:, 4, :], in0=stt[:, 1, :], scalar=1.0 / 128, in1=stt[:, 3, :],
                                                             op0=ALU.mult, op1=ALU.subtract), reads=[t_st], writes=[t_st])
                P.op("dve", lambda e: e.tensor_scalar(out=stt[:, 5, :], in0=stt[:, 4, :], scalar1=EPS, scalar2=None,
                                                      op0=ALU.add), reads=[t_st], writes=[t_st])
                P.op("act", lambda e: e.activation(out=stt[:, 4, :], in_=stt[:, 5, :], func=AF.Sqrt), reads=[t_st], writes=[t_st])
                P.op("dve", lambda e: e.reciprocal(out=stt[:, 5, :], in_=stt[:, 4, :]), reads=[t_st], writes=[t_st])
                alltmp = list(t_tmp)
                P.op("dve", lambda e: e.tensor_tensor(out=tmp[:, :, :], in0=r[:, :, :], in1=bc(stt[:, 2, :], [128, 16, 128]),
                                                      op=ALU.subtract), reads=allr + [t_st], writes=alltmp)
                P.op("pool", lambda e: e.tensor_tensor(out=tmp[:, :, :], in0=tmp[:, :, :], in1=bc(stt[:, 5, :], [128, 16, 128]),
                                                       op=ALU.mult), reads=[t_st], writes=alltmp)
                tmp4 = tmp[:, :, :].rearrange("p (c h) e -> p c (h e)", h=4)
                P.op("pool", lambda e, hg=hg: e.tensor_tensor(
                    out=tmp4, in0=tmp4, in1=gng[:, hg * 512:hg * 512 + 512].unsqueeze(1).to_broadcast([128, 4, 512]),
                    op=ALU.mult), reads=[t_tab], writes=alltmp)
                P.op("dve", lambda e: e.tensor_tensor(out=rtk[:, :, :], in0=tmp4, in1=sg[sgi][:, :, :], op=ALU.mult),
                     reads=alltmp + [t_sg[sgi]], writes=[t_rtk])
    def back_tr(hg, j, rs_):
                for h in range(4):
                    for cc in range(4):
                        P.op("pe", lambda e, h=h, cc=cc: e.transpose(tr[:, cc, :], rtk[:, cc, h * 128:h * 128 + 128], ident[:, :]),
                             reads=[t_rtk, t_id], writes=[t_tr])
                    P.op("act", lambda e, h=h: e.activation(out=retT[rs_][:, h, :].rearrange("p (c t) -> p c t", t=128),
                                                           in_=tr[:, :, :], func=AF.Copy),
                         reads=[t_tr], writes=[t_rT[rs_]])
                P.op("sp", lambda e: e.dma_start(out=mixv[:, hg * 4:hg * 4 + 4, j * 512:j * 512 + 512],
                                                 in_=retT[rs_][:, :, :]),
                     reads=[t_rT[rs_]], dsem=f"mx{rs_}")

    return loads, units, front_b, back_gn, back_tr


def phase1a(nc, Dm):
    with contextlib.ExitStack() as st:
        P = Prog(nc)
        T = lambda n, s, d: st.enter_context(nc.sbuf_tensor(n, s, d))
        PS = lambda n, s, d=F32: st.enter_context(nc.psum_tensor(n, s, d))
        wa = T("a_w", [128, 16, 1536], BF16)
        xt = [T(f"a_xt{i}", [128, 16, 512], BF16) for i in range(2)]
        kaT = T("a_kT", [128, 6, 4, 512], BF16)
        va = T("a_v", [128, 24, 4, 130], BF16)
        qaT = T("a_qT", [128, 4, 4, 512], BF16)
        am = T("a_m", [128, 4, 17, 128], BF16)
        E = [T(f"a_E{i}", [128, 512], F32) for i in range(4)]
        Pb = [T(f"a_P{i}", [128, 512], BF16) for i in range(4)]
        otk = [T(f"a_otk{i}", [128, 4, 512], BF16) for i in range(2)]
        attT = [T(f"a_aT{i}", [128, 4, 512], BF16) for i in range(2)]
        valid = T("a_val", [128, 48], F32)
        rec = T("a_rec", [128, 4], F32)
        ident, t_id = mk_ident(P, nc, st, "a_id")
        pq = [PS(f"a_pq{i}", [128, 512]) for i in range(2)]
        scp = [PS(f"a_sc{i}", [128, 512]) for i in range(3)]
        accb = [PS(f"a_acc{i}", [128, 512]) for i in range(2)]
        acc = [accb[t // 2][:, (t % 2) * 256:(t % 2) * 256 + 129] for t in range(4)]
        tr = PS("a_tr", [128, 4, 128], BF16)
        t_w = [Tok() for _ in range(3)]
        t_x = [Tok(), Tok()]
        t_kT = [Tok() for _ in range(6)]
        t_v = [Tok() for _ in range(6)]
        t_qT = [Tok() for _ in range(4)]
        t_am, t_val = Tok(), Tok()
        t_E = [Tok() for _ in range(4)]
        t_Pb = [Tok() for _ in range(4)]
        t_otk, t_rec = [Tok(), Tok()], Tok()
        t_aT = [Tok(), Tok()]
        t_pq, t_tr = [Tok(), Tok()], Tok()
        t_sc = [Tok(), Tok(), Tok()]
        t_accb = [Tok(), Tok()]
        t_acc = [t_accb[t // 2] for t in range(4)]
        pqi = [0]
        pend = [None]
        P.op("sp", lambda e: e.dma_start(out=valid[:, :], in_=Dm["valid"]), writes=[t_val], dsem="val")
        win = kview(Dm["w_in"])
        xw = kview(Dm["xT_own"])
        mixv = kview(Dm["mixT"])
        am_src = Dm["amask"].rearrange("p (h o q) -> p h o q", h=8, o=17)
        tp = [0]
        rr = [0]
        ao = [0]
        def proj(lhsT_fn, rhs_fn, reads, evac):
            pi_ = pqi[0] % 2
            pqi[0] += 1
            for kc in range(16):
                P.op("pe", lambda e, kc=kc: e.matmul(pq[pi_][:, :], lhsT_fn(kc), rhs_fn(kc), start=(kc == 0), stop=(kc == 15)),
                     reads=reads, writes=[t_pq[pi_]])
            evac(pq[pi_], t_pq[pi_])

        def loads(hg, i, xs):
            P.op("pool", lambda e: e.dma_start(out=xt[xs][:, :, :], in_=xw[:, :, i * 512:(i + 1) * 512]),
                 writes=[t_x[xs]], dsem=f"x{xs}")

        def units(hg, i, xs):
            sl = i % 6
            us = []
            for h in range(4):
                us.append(lambda h=h: proj(
                    lambda kc: wa[:, kc, 512 + h * 128:512 + h * 128 + 128], lambda kc: xt[xs][:, kc, :], [t_w[1], t_x[xs]],
                    lambda pp, tpp: P.op("act", lambda e: e.activation(out=kaT[:, sl, h, :], in_=pp[:, :], func=AF.Copy),
                                         reads=[tpp], writes=[t_kT[sl]])))

            def v_unit(cc):
                blk = sl * 4 + cc
                proj(lambda kc: xt[xs][:, kc, cc * 128:cc * 128 + 128], lambda kc: wa[:, kc, 1024:1536], [t_w[2], t_x[xs]],
                     lambda pp, tpp: P.op("dve", lambda e: e.tensor_copy(va[:, blk, :, 0:128],
                                                                        pp[:, :].rearrange("p (h e) -> p h e", e=128)),
                                          reads=[tpp], writes=[t_v[sl]]))
                P.op("pool", lambda e: e.tensor_copy(
                    va[:, blk, :, 128:129], valid[:, i * 4 + cc:i * 4 + cc + 1].unsqueeze(1).to_broadcast([128, 4, 1])),
                    reads=[t_val], writes=[t_v[sl]])
            for cc in range(4):
                us.append(lambda cc=cc: v_unit(cc))
            if 2 <= i <= 9:
                qsw = i % 4
                for h in range(4):
                    us.append(lambda h=h: proj(
                        lambda kc: wa[:, kc, h * 128:h * 128 + 128], lambda kc: xt[xs][:, kc, :], [t_w[0], t_x[xs]],
                        lambda pp, tpp: P.op("act", lambda e: e.activation(out=qaT[:, qsw, h, :], in_=pp[:, :], func=AF.Copy),
                                             reads=[tpp], writes=[t_qT[qsw]])))
            return us

        def att(hg, i, oi, extra, mid):
                qs = (i - 2) % 4
                OMAX = (1, 2, 4, 8, 8, 8, 8, 8)
                items = []
                for h in range(4):
                    om = OMAX[hg * 4 + h]
                    for rk in range(20):
                        t_lo, t_hi = max(0, rk - 8 - om), min(3, rk - 8 + om)
                        if t_lo > t_hi:
                            continue
                        items.append((h, rk, t_lo, t_hi, om))

                def stage_a(n):
                    h, rk, t_lo, t_hi, om = items[n]
                    ncol = (t_hi - t_lo + 1) * 128
                    ksl = (i - 4 + rk // 4) % 6
                    kc4 = rk % 4
                    b2 = n % 3
                    b3 = n % 4
                    q0 = t_lo * 128
                    P.op("pe", lambda e: e.matmul(
                        scp[b2][:, 0:ncol], kaT[:, ksl, h, kc4 * 128:kc4 * 128 + 128], qaT[:, qs, h, q0:q0 + ncol],
                        start=True, stop=True), reads=[t_kT[ksl], t_qT[qs]], writes=[t_sc[b2]])
                    P.op("act", lambda e: e.activation(out=E[b3][:, 0:ncol], in_=scp[b2][:, 0:ncol], func=AF.Exp, scale=SC),
                         reads=[t_sc[b2]], writes=[t_E[b3]])
                    o_lo = 16 + t_lo - rk
                    nt = t_hi - t_lo + 1
                    P.op("dve", lambda e: e.tensor_tensor(
                        out=Pb[b3][:, 0:ncol].rearrange("p (o q) -> p o q", q=128),
                        in0=E[b3][:, 0:ncol].rearrange("p (o q) -> p o q", q=128),
                        in1=am[:, h, o_lo:o_lo + nt, :], op=ALU.mult),
                        reads=[t_E[b3], t_am], writes=[t_Pb[b3]])

                def stage_b(n):
                    h, rk, t_lo, t_hi, om = items[n]
                    ksl = (i - 4 + rk // 4) % 6
                    blk = ksl * 4 + rk % 4
                    b3 = n % 4
                    for t in range(t_lo, t_hi + 1):
                        P.op("pe", lambda e, t=t: e.matmul(
                            acc[t], Pb[b3][:, (t - t_lo) * 128:(t - t_lo) * 128 + 128], va[:, blk, h, 0:129],
                            start=(t % 2 == 0 and rk == 8 + t - om), stop=(rk == 8 + t + om), skip_group_check=True),
                            reads=[t_Pb[b3], t_v[ksl]], writes=[t_acc[t]])
                    if n + 1 == len(items) or items[n + 1][0] != h:
                        for t in range(4):
                            P.op("dve", lambda e, t=t: e.reciprocal(out=rec[:, t:t + 1], in_=acc[t][:, 128:129]),
                                 reads=[t_acc[t]], writes=[t_rec])
                            P.op("dve", lambda e, t=t: e.tensor_scalar(out=otk[oi][:, t, h * 128:h * 128 + 128], in0=acc[t][:, 0:128],
                                                                      scalar1=rec[:, t:t + 1], scalar2=None, op0=ALU.mult),
                                 reads=[t_acc[t], t_rec], writes=[t_otk[oi]])

                SK = 2
                every = max(1, len(items) // (len(extra) + 1)) if extra else 0
                for n in range(len(items) + SK):
                    if n < len(items):
                        stage_a(n)
                    if n == 3 and mid is not None:
                        mid()
                    if extra and n > 0 and n % every == 0:
                        extra.pop(0)()
                    if n >= SK:
                        stage_b(n - SK)
                while extra:
                    extra.pop(0)()

        def back(hg, j, oi):
                a_ = ao[0] % 2
                ao[0] += 1
                for h in range(4):
                    for t in range(4):
                        P.op("pe", lambda e, h=h, t=t: e.transpose(tr[:, t, :], otk[oi][:, t, h * 128:h * 128 + 128], ident[:, :]),
                             reads=[t_otk[oi], t_id], writes=[t_tr])
                    P.op("act", lambda e, h=h, a_=a_: e.activation(out=attT[a_][:, h, :].rearrange("p (c t) -> p c t", t=128),
                                                                  in_=tr[:, :, :], func=AF.Copy),
                         reads=[t_tr], writes=[t_aT[a_]])
                P.op("sp", lambda e: e.dma_start(out=mixv[:, 8 + hg * 4:8 + hg * 4 + 4, j * 512:j * 512 + 512],
                                                 in_=attT[a_][:, :, :]),
                     reads=[t_aT[a_]], dsem=f"mx{a_}")

        for hg in range(2):
            for seg in range(3):
                c0 = 4096 + seg * 1024 + hg * 512
                P.op("pool", lambda e, seg=seg, c0=c0: e.dma_start(out=wa[:, :, seg * 512:seg * 512 + 512],
                                                                 in_=win[:, :, c0:c0 + 512]),
                     writes=[t_w[seg]], dsem=f"w{seg}")
            P.op("pool", lambda e, hg=hg: e.dma_start(out=am[:, :, :, :], in_=am_src[:, hg * 4:hg * 4 + 4, :, :]),
                 writes=[t_am], dsem="am")
            loads(hg, 0, tp[0] % 2)
            for u_ in units(hg, 0, tp[0] % 2):
                u_()
            for i in range(12):
                xs = tp[0] % 2
                tp[0] += 1
                extra = []
                if i + 1 < 12:
                    loads(hg, i + 1, 1 - xs)
                    extra = units(hg, i + 1, 1 - xs)
                if i >= 4:
                    mid = None
                    if pend[0] is not None:
                        pv = pend[0]
                        mid = lambda pv=pv: back(*pv)
                        pend[0] = None
                    oi = i % 2
                    att(hg, i, oi, extra, mid)
                    pend[0] = (hg, i - 4, oi)
                else:
                    for u_ in extra:
                        u_()
        back(*pend[0])
        P.emit()


def phase2(nc, Dm):
    with contextlib.ExitStack() as st:
        P = Prog(nc)
        T = lambda n, s, d: st.enter_context(nc.sbuf_tensor(n, s, d))
        PS = lambda n, s, d=F32: st.enter_context(nc.psum_tensor(n, s, d))
        y = T("f_y", [128, 8, 2048], F32)
        bufs = [T("f_A", [128, 16, 1024], BF16), T("f_B", [128, 16, 1024], BF16)]
        slab = [T(f"f_sl{i}", [128, 16, 512], BF16) for i in range(3)]
        lng = T("f_lng", [128, 2048], F32)
        lnb = T("f_lnb", [128, 2048], F32)
        hb = T("f_hb", [128, 2048], BF16)
        rl = [T(f"f_rl{i}", [128, 512], F32) for i in range(3)]
        bst = T("f_bst", [128, 8, 24], F32)
        mv = T("f_mv", [128, 8, 4], F32)
        ident, t_id = mk_ident(P, nc, st, "f_id")
        ps = [PS(f"f_ps{i}", [128, 512]) for i in range(6)]
        tr = [PS(f"f_tr{i}", [128, 4, 128], BF16) for i in range(2)]
        t_y = [[Tok() for _ in range(4)] for _ in range(8)]
        t_buf = [[[Tok(), Tok()] for _ in range(16)] for _ in range(2)]
        t_sl = [Tok() for _ in range(3)]
        t_ln, t_hb = Tok(), Tok()
        t_mv = [Tok() for _ in range(8)]
        t_rl = [Tok(), Tok(), Tok()]
        t_ps = [Tok() for _ in range(6)]
        t_tr = [Tok(), Tok()]
        mixv = kview(Dm["mixT"])
        si = [0]
        pi = [0]
        ri = [0]
        ti = [0]

        def load_slab(src):
            s = si[0] % 3
            si[0] += 1
            P.op("sp", lambda e: e.dma_start(out=slab[s][:, :, :], in_=src), writes=[t_sl[s]], dsem=f"sl{s}")
            return s

        def ln_tables(gname, bname):
            P.op("sp", lambda e: e.dma_start(out=lng[:, :], in_=Dm[gname]), writes=[t_ln], dsem="ln")
            P.op("sp", lambda e: e.dma_start(out=lnb[:, :], in_=Dm[bname]), writes=[t_ln], dsem="ln")

        def ln_stats(tb):
            tm = t_mv[tb]
            for c in range(4):
                P.op("dve", lambda e, c=c: e.bn_stats(out=bst[:, tb, c * 6:c * 6 + 6], in_=y[:, tb, c * 512:c * 512 + 512]),
                     reads=[t_y[tb][c]], writes=[tm])
            P.op("dve", lambda e: e.bn_aggr(out=mv[:, tb, 0:2], in_=bst[:, tb, :]), reads=[tm], writes=[tm])
            P.op("dve", lambda e: e.tensor_scalar(out=mv[:, tb, 2:3], in0=mv[:, tb, 1:2], scalar1=EPS, scalar2=None, op0=ALU.add),
                 reads=[tm], writes=[tm])
            P.op("act", lambda e: e.activation(out=mv[:, tb, 3:4], in_=mv[:, tb, 2:3], func=AF.Sqrt), reads=[tm], writes=[tm])
            P.op("dve", lambda e: e.reciprocal(out=mv[:, tb, 2:3], in_=mv[:, tb, 3:4]), reads=[tm], writes=[tm])
            P.op("dve", lambda e: e.scalar_tensor_tensor(out=mv[:, tb, 3:4], in0=mv[:, tb, 0:1], scalar=-1.0, in1=mv[:, tb, 2:3],
                                                         op0=ALU.mult, op1=ALU.mult), reads=[tm], writes=[tm])

        def ln_norm(tb):
            tm = t_mv[tb]
            P.op("act", lambda e: e.activation(out=y[:, tb, :], in_=y[:, tb, :], func=AF.Identity, bias=mv[:, tb, 3:4],
                                               scale=mv[:, tb, 2:3]), reads=[tm], writes=t_y[tb])

        def ln_gb(tb):
            P.op("dve", lambda e: e.tensor_tensor(out=y[:, tb, :], in0=y[:, tb, :], in1=lng[:, :], op=ALU.mult),
                 reads=[t_ln], writes=t_y[tb])
            P.op("dve", lambda e: e.tensor_tensor(out=y[:, tb, :], in0=y[:, tb, :], in1=lnb[:, :], op=ALU.add),
                 reads=[t_ln], writes=t_y[tb])

        def load_mix(tt):
            X = tt % 2
            P.op("pool", lambda e: e.dma_start(out=bufs[X][:, :, :], in_=mixv[:, :, tt * 1024:(tt + 1) * 1024]),
                 writes=[tk for kc in range(16) for tk in t_buf[X][kc]], dsem=f"A{X}")

        def tile(tt):
            X, Yb = tt % 2, 1 - tt % 2
            bX, bY = bufs[X], bufs[Yb]
            tX, tY = t_buf[X], t_buf[Yb]
            if tt == 0:
                load_mix(0)
            for tb in range(8):
                r0 = tt * 1024 + tb * 128
                P.op("pool", lambda e, tb=tb, r0=r0: e.dma_start(out=y[:, tb, :], in_=Dm["x_own"][r0:r0 + 128, :]),
                     writes=t_y[tb], dsem=f"y{tb}")
            for cg in range(4):
                s = load_slab(kview(Dm["w_out_b"])[:, :, cg * 512:cg * 512 + 512])
                for tb in range(8):
                    p = pi[0] % 6
                    pi[0] += 1
                    for kc in range(16):
                        P.op("pe", lambda e, kc=kc, tb=tb, s=s, p=p: e.matmul(
                            ps[p][:, :], bX[:, kc, tb * 128:tb * 128 + 128], slab[s][:, kc, :], start=(kc == 0), stop=(kc == 15)),
                            reads=[tX[kc][tb // 4], t_sl[s]], writes=[t_ps[p]])
                    P.op("dve", lambda e, tb=tb, cg=cg, p=p: e.scalar_tensor_tensor(
                        out=y[:, tb, cg * 512:cg * 512 + 512], in0=y[:, tb, cg * 512:cg * 512 + 512], scalar=ALPHA,
                        in1=ps[p][:, :], op0=ALU.mult, op1=ALU.add), reads=[t_ps[p]], writes=[t_y[tb][cg]])
            ln_tables("ln1g", "ln1b")
            ln_stats(0)
            ln_stats(1)
            ln_norm(0)
            for tb in range(8):
                if tb + 2 < 8:
                    ln_stats(tb + 2)
                if tb + 1 < 8:
                    ln_norm(tb + 1)
                ln_gb(tb)
                P.op("act", lambda e, tb=tb: e.activation(out=hb[:, :], in_=y[:, tb, :], func=AF.Copy),
                     reads=t_y[tb], writes=[t_hb])
                for g4 in range(4):
                    q = ti[0] % 2
                    ti[0] += 1
                    for m in range(4):
                        kc = g4 * 4 + m
                        P.op("pe", lambda e, kc=kc, m=m, q=q: e.transpose(tr[q][:, m, :], hb[:, kc * 128:kc * 128 + 128], ident[:, :]),
                             reads=[t_hb, t_id], writes=[t_tr[q]])
                    P.op("act", lambda e, g4=g4, tb=tb, q=q: e.activation(out=bY[:, g4 * 4:g4 * 4 + 4, tb * 128:tb * 128 + 128],
                                                                         in_=tr[q][:, :, :], func=AF.Copy),
                         reads=[t_tr[q]], writes=[tY[g4 * 4 + m][tb // 4] for m in range(4)])
            for u in range(4):
                for sidx in range(4):
                    c0 = u * 2048 + sidx * 512
                    s = load_slab(kview(Dm["w_up_b"])[:, :, c0:c0 + 512])
                    for m in range(4):
                        for th in range(2):
                            p = pi[0] % 6
                            pi[0] += 1
                            for kc in range(16):
                                P.op("pe", lambda e, kc=kc, m=m, th=th, s=s, p=p: e.matmul(
                                    ps[p][:, :], slab[s][:, kc, m * 128:m * 128 + 128], bY[:, kc, th * 512:th * 512 + 512],
                                    start=(kc == 0), stop=(kc == 15)),
                                    reads=[t_sl[s], tY[kc][th]], writes=[t_ps[p]])
                            q = ri[0] % 3
                            ri[0] += 1
                            fc = sidx * 4 + m
                            P.op("act", lambda e, p=p, q=q: e.activation(out=rl[q][:, :], in_=ps[p][:, :], func=AF.Relu),
                                 reads=[t_ps[p]], writes=[t_rl[q]])
                            P.op("dve", lambda e, q=q, fc=fc, th=th: e.tensor_tensor(
                                out=bX[:, fc, th * 512:th * 512 + 512], in0=rl[q][:, :], in1=rl[q][:, :], op=ALU.mult),
                                reads=[t_rl[q]], writes=[tX[fc][th]])
                if u == 3 and tt + 1 < 4:
                    load_mix(tt + 1)
                for cg in range(4):
                    s = load_slab(kview(Dm["w_down_b"][u * 2048:(u + 1) * 2048, :])[:, :, cg * 512:cg * 512 + 512])
                    for tb in range(8):
                        p = pi[0] % 6
                        pi[0] += 1
                        for kc in range(16):
                            P.op("pe", lambda e, kc=kc, tb=tb, s=s, p=p: e.matmul(
                                ps[p][:, :], bX[:, kc, tb * 128:tb * 128 + 128], slab[s][:, kc, :], start=(kc == 0), stop=(kc == 15)),
                                reads=[tX[kc][tb // 4], t_sl[s]], writes=[t_ps[p]])
                        P.op("dve", lambda e, tb=tb, cg=cg, p=p, u=u: e.scalar_tensor_tensor(
                            out=y[:, tb, cg * 512:cg * 512 + 512], in0=y[:, tb, cg * 512:cg * 512 + 512],
                            scalar=(ALPHA if u == 0 else 1.0), in1=ps[p][:, :], op0=ALU.mult, op1=ALU.add),
                            reads=[t_ps[p]], writes=[t_y[tb][cg]])
            ln_tables("ln2g", "ln2b")
            ln_stats(0)
            ln_stats(1)
            ln_norm(0)
            for tb in range(8):
                if tb + 2 < 8:
                    ln_stats(tb + 2)
                if tb + 1 < 8:
                    ln_norm(tb + 1)
                ln_gb(tb)
                r0 = tt * 1024 + tb * 128
                P.op("sp", lambda e, tb=tb, r0=r0: e.dma_start(out=Dm["out"][r0:r0 + 128, :], in_=y[:, tb, :]),
                     reads=t_y[tb], dsem=f"o{tb}")

        for tt in range(4):
            tile(tt)
        P.emit()


def build_program(debug=False):
    nc = bass.Bass("TRN2", target_bir_lowering=False)
    Dm = {}

    def inp(name, shape):
        Dm[name] = nc.dram_tensor(name, shape, F32, kind="ExternalInput").ap()

    inp("xT_own", [D, 6144])
    inp("xT_oth", [D, 12288])
    inp("x_own", [NTOK, D])
    inp("w_in", [D, 7168])
    inp("w_out", [D, D])
    inp("w_up", [D, 8192])
    inp("w_down", [8192, D])
    inp("rs_oth", [128, 96, 16])
    inp("af_oth", [128, 96, 8])
    inp("own_tab", [128, 48])
    inp("rmask", [128, 2, 4, 128])
    inp("gng", [128, 1024])
    inp("amask", [128, 8 * 17 * 128])
    inp("valid", [128, 48])
    for n in ("ln1g", "ln1b", "ln2g", "ln2b"):
        inp(n, [128, D])
    Dm["out"] = nc.dram_tensor("out", [NTOK, D], F32, kind="ExternalOutput").ap()
    k = "ExternalOutput" if debug else "Internal"
    Dm["mixT"] = nc.dram_tensor("mixT", [D, NTOK], BF16, kind=k).ap()
    Dm["sb_bound"] = nc.dram_tensor("sb_bound", [8, 128, 8, 128], F32, kind=k).ap()
    Dm["sf_in"] = nc.dram_tensor("sf_in", [128, 8, 128], F32, kind=k).ap()
    for n in ("own_k", "own_vf", "own_vb"):
        Dm[n] = nc.dram_tensor(n, [32, 128, 1024], BF16, kind="Internal").ap()
    Dm["w_out_b"] = nc.dram_tensor("w_out_b", [D, D], BF16, kind="Internal").ap()
    Dm["w_up_b"] = nc.dram_tensor("w_up_b", [D, 8192], BF16, kind="Internal").ap()
    Dm["w_down_b"] = nc.dram_tensor("w_down_b", [8192, D], BF16, kind="Internal").ap()
    phase0(nc, Dm)
    phase1r(nc, Dm)
    phase1a(nc, Dm)
    phase2(nc, Dm)
    return nc


def _const_tables(c):
    hh = np.arange(8, dtype=np.float64)
    lgf = np.log1p(-np.exp2(-5.0 - hh))
    lgb = np.log1p(-np.exp2(-5.5 - hh))
    p = np.arange(128, dtype=np.float64)[:, None]
    rsf = SC * np.exp(lgf[None, :] * (127.0 - p))
    rsb = SC * np.exp(lgb[None, :] * p)
    osf = np.exp(lgf[None, :] * (p + 1.0 - 128.0))
    osb = np.exp(-lgb[None, :] * p)
    gf = np.broadcast_to(np.exp(128.0 * lgf)[None, :], (128, 8))
    gb = np.broadcast_to(np.exp(128.0 * lgb)[None, :], (128, 8))
    own_tab = np.concatenate([rsf, rsb, osf, osb, gf, gb], axis=1).astype(np.float32)
    nbefore = c * 32
    rs_oth = np.zeros((128, 96, 16), np.float64)
    af_oth = np.ones((128, 96, 8), np.float64)
    rs_oth[:, :nbefore, 0:8] = rsf[:, None, :]
    rs_oth[:, nbefore:, 8:16] = rsb[:, None, :]
    af_oth[:, :nbefore, :] = gf[:, None, :]
    kk = np.arange(128)[:, None]
    qq = np.arange(128)[None, :]
    mf = (kk <= qq).astype(np.float32)
    mb = (kk > qq).astype(np.float32)
    rmask = np.stack([np.broadcast_to(mf[:, None, :], (128, 4, 128)), np.broadcast_to(mb[:, None, :], (128, 4, 128))], axis=1)
    return own_tab, rs_oth.astype(np.float32), af_oth.astype(np.float32), np.ascontiguousarray(rmask, dtype=np.float32)


def _amask():
    slopes = np.exp2(-8.0 * (np.arange(8, dtype=np.float64) + 1.0) / 8.0)
    k = np.arange(128)[:, None, None]
    o = (np.arange(17) - 8)[None, :, None]
    q = np.arange(128)[None, None, :]
    delta = np.abs(o * 128 + q - k)
    mult = ((delta <= 64).astype(np.float64) + ((delta % 4 == 0) & (delta <= 256)) + ((delta % 16 == 0) & (delta <= 1024)))
    tab = mult[:, None, :, :] * np.exp(-slopes[None, :, None, None] * delta[:, None, :, :])
    return np.ascontiguousarray(tab.reshape(128, 8 * 17 * 128), dtype=np.float32)


_CACHE = {}


def kernel(x, w_in, ret_gn_gain, w_out, ln1_gain, ln1_bias, w_up, w_down, ln2_gain, ln2_bias, _debug=False):
    x = np.asarray(x, np.float32)
    w_in0 = np.ascontiguousarray(np.asarray(w_in, np.float32)[0])
    w_out0 = np.ascontiguousarray(np.asarray(w_out, np.float32)[0])
    w_up0 = np.ascontiguousarray(np.asarray(w_up, np.float32)[0])
    w_down0 = np.ascontiguousarray(np.asarray(w_down, np.float32)[0])
    rep = lambda v, n: np.ascontiguousarray(np.broadcast_to(np.asarray(v, np.float32).reshape(1, n), (128, n)))
    gng = rep(ret_gn_gain, 1024)
    l1g, l1b, l2g, l2b = rep(ln1_gain, D), rep(ln1_bias, D), rep(ln2_gain, D), rep(ln2_bias, D)
    amask = _amask()
    in_maps = []
    for core in range(8):
        b, c = core // 4, core % 4
        t0 = c * NTOK
        xb = x[b]
        xT = np.ascontiguousarray(xb.T)
        xT_own = np.zeros((D, 6144), np.float32)
        lo, hi = t0 - 1024, t0 + NTOK + 1024
        slo, shi = max(lo, 0), min(hi, S_LEN)
        xT_own[:, slo - lo:shi - lo] = xT[:, slo:shi]
        pos = np.arange(lo, hi)
        valid = ((pos >= 0) & (pos < S_LEN)).astype(np.float32).reshape(48, 128).T
        before = xT[:, :t0]
        after = xT[:, t0 + NTOK:]
        na = after.shape[1] // 128
        after_r = after.reshape(D, na, 128)[:, ::-1, :].reshape(D, na * 128)
        xT_oth = np.ascontiguousarray(np.concatenate([before, after_r], axis=1))
        own_tab, rs_oth, af_oth, rmask = _const_tables(c)
        in_maps.append({
            "xT_own": xT_own, "xT_oth": xT_oth, "x_own": np.ascontiguousarray(xb[t0:t0 + NTOK]),
            "w_in": w_in0, "w_out": w_out0, "w_up": w_up0, "w_down": w_down0,
            "rs_oth": rs_oth, "af_oth": af_oth, "own_tab": own_tab, "rmask": rmask, "gng": gng,
            "amask": amask, "valid": np.ascontiguousarray(valid),
            "ln1g": l1g, "ln1b": l1b, "ln2g": l2g, "ln2b": l2b,
        })
    key = bool(_debug)
    if key not in _CACHE:
        _CACHE[key] = build_program(debug=_debug)
    nc = _CACHE[key]
    res = run_bass_kernel_spmd(nc, in_maps, core_ids=list(range(8)))
    out = np.zeros((2, S_LEN, D), np.float32)
    for core in range(8):
        b, c = core // 4, core % 4
        out[b, c * NTOK:(c + 1) * NTOK] = res.results[core]["out"]
    if _debug:
        return out, res
    return out
```

```python
import contextlib
import math
import numpy as np
import concourse.bass as bass
import concourse.mybir as mybir
from concourse.bass_utils import run_bass_kernel_spmd

F32 = mybir.dt.float32
BF16 = mybir.dt.bfloat16
AF = mybir.ActivationFunctionType
ALU = mybir.AluOpType
AX = mybir.AxisListType

S_LEN = 16384
D = 2048
NTOK = 4096
ALPHA = 2.0 ** 0.25
EPS = 1e-5
SC = 128.0 ** -0.5
ENGS = ("pe", "act", "dve", "pool", "sp")


class Tok:
    __slots__ = ("w", "r")

    def __init__(self):
        self.w = None
        self.r = {}


class Op:
    __slots__ = ("eng", "fn", "deps", "key", "idx", "milestone", "count", "dsem")

    def __init__(self, eng, fn, key, idx, dsem):
        self.eng, self.fn, self.key, self.idx, self.dsem = eng, fn, key, idx, dsem
        self.deps = {}
        self.milestone = False
        self.count = None


_PROG_ID = [0]


class Prog:
    def __init__(self, nc):
        self.nc = nc
        _PROG_ID[0] += 1
        self.pid = _PROG_ID[0]
        self.ops = {e: [] for e in ENGS}
        self.keyidx = {}
        self.last = {}

    def op(self, eng, fn, reads=(), writes=(), dsem=None):
        key = eng if dsem is None else ("dma", dsem)
        idx = self.keyidx.get(key, 0)
        self.keyidx[key] = idx + 1
        o = Op(eng, fn, key, idx, dsem)
        deps = o.deps

        def need(p):
            if p is None:
                return
            if p.key == "pe" and key == "pe":
                return
            q = deps.get(p.key)
            if q is None or q.idx < p.idx:
                deps[p.key] = p

        for t in reads:
            need(t.w)
        for t in writes:
            need(t.w)
            for p in t.r.values():
                need(p)
        for t in reads:
            t.r[key] = o
        for t in writes:
            t.w = o
            t.r = {}
        self.ops[eng].append(o)
        self.last[key] = o
        return o

    def emit(self):
        nc = self.nc
        finals = [o for k, o in self.last.items() if isinstance(k, tuple)]
        for e in ENGS:
            for o in self.ops[e]:
                for p in o.deps.values():
                    p.milestone = True
        cnt = {}
        for e in ENGS:
            for o in self.ops[e]:
                if o.dsem is not None:
                    cnt[o.key] = cnt.get(o.key, 0) + 16
                    o.count = cnt[o.key]
                elif o.milestone:
                    cnt[o.key] = cnt.get(o.key, 0) + 1
                    o.count = cnt[o.key]
        with contextlib.ExitStack() as stack:
            sems = {}
            for e in ENGS:
                sems[e] = stack.enter_context(nc.semaphore(f"s{self.pid}_{e}"))
            for k in self.keyidx:
                if isinstance(k, tuple):
                    sems[k] = stack.enter_context(nc.semaphore(f"d{self.pid}_{k[1]}"))
            block = stack.enter_context(nc.Block(f"blk{self.pid}"))
            prog = self

            def run(e, eng):
                waited = {}
                for o in prog.ops[e]:
                    for k, p in o.deps.items():
                        if waited.get(k, 0) >= p.count:
                            continue
                        eng.wait_ge(sems[k], p.count)
                        waited[k] = p.count
                    ins = o.fn(eng)
                    if o.dsem is not None:
                        ins.then_inc(sems[o.key], 16)
                    elif o.milestone:
                        ins.then_inc(sems[e], 1)
                if e == "sp":
                    for p in finals:
                        if waited.get(p.key, 0) < p.count:
                            eng.wait_ge(sems[p.key], p.count)

            block.tensor(lambda eng: run("pe", eng))
            block.scalar(lambda eng: run("act", eng))
            block.vector(lambda eng: run("dve", eng))
            block.gpsimd(lambda eng: run("pool", eng))
            block.sync(lambda eng: run("sp", eng))


def bc(ap, shape):
    return ap.unsqueeze(2).to_broadcast(shape)


def mk_ident(P, nc, st, name):
    idf = st.enter_context(nc.sbuf_tensor(name + "f", [128, 128], F32))
    idb = st.enter_context(nc.sbuf_tensor(name, [128, 128], BF16))
    t = Tok()
    P.op("pool", lambda e: e.memset(idf[:, :], 0.0), writes=[t])
    P.op("pool", lambda e: e.affine_select(out=idf[:, :], in_=idf[:, :], pattern=[[-1, 128]],
                                           compare_op=ALU.not_equal, fill=1.0, base=0,
                                           channel_multiplier=1), reads=[t], writes=[t])
    P.op("dve", lambda e: e.tensor_copy(idb[:, :], idf[:, :]), reads=[t], writes=[t])
    return idb, t


def kview(ap2d):
    return ap2d.rearrange("(k p) n -> p k n", p=128)


def phase0(nc, Dm):
    with contextlib.ExitStack() as st:
        P = Prog(nc)
        T = lambda n, s, d: st.enter_context(nc.sbuf_tensor(n, s, d))
        PS = lambda n, s, d=F32: st.enter_context(nc.psum_tensor(n, s, d))
        wkv = T("p0_wkv", [128, 16, 2048], BF16)
        xt = [T(f"p0_xt{i}", [128, 16, 512], BF16) for i in range(2)]
        ksb = [T(f"p0_k{i}", [128, 1024], BF16) for i in range(2)]
        vf = [T(f"p0_vf{i}", [128, 8, 128], BF16) for i in range(2)]
        vb = [T(f"p0_vb{i}", [128, 8, 128], BF16) for i in range(2)]
        Sf = T("p0_Sf", [128, 8, 128], F32)
        Sb = T("p0_Sb", [128, 8, 128], F32)
        rs_oth = T("p0_rs", [128, 96, 16], F32)
        af_oth = T("p0_af", [128, 96, 8], F32)
        otab = T("p0_ot", [128, 48], F32)
        kp = PS("p0_kp", [128, 1024])
        vp = PS("p0_vp", [128, 1024])
        kvf = PS("p0_kvf", [128, 8, 128])
        kvb = PS("p0_kvb", [128, 8, 128])
        t_w = [Tok() for _ in range(4)]
        t_x = [Tok(), Tok()]
        t_ks = [Tok(), Tok()]
        t_vf = [Tok(), Tok()]
        t_vb = [Tok(), Tok()]
        t_Sf, t_Sb, t_tab = Tok(), Tok(), Tok()
        t_kp, t_vp, t_kvf, t_kvb = Tok(), Tok(), Tok(), Tok()
        win = kview(Dm["w_in"])
        for g in range(4):
            P.op("pool", lambda e, g=g: e.dma_start(out=wkv[:, 4 * g:4 * g + 4, :],
                                                  in_=win[:, 4 * g:4 * g + 4, 1024:3072]),
                 writes=[t_w[g]], dsem=f"w{g}")
        P.op("sp", lambda e: e.dma_start(out=rs_oth[:, :, :], in_=Dm["rs_oth"]), writes=[t_tab], dsem="tab")
        P.op("sp", lambda e: e.dma_start(out=af_oth[:, :, :], in_=Dm["af_oth"]), writes=[t_tab], dsem="tab")
        P.op("sp", lambda e: e.dma_start(out=otab[:, :], in_=Dm["own_tab"]), writes=[t_tab], dsem="tab")
        P.op("dve", lambda e: e.memset(Sf[:, :, :], 0.0), writes=[t_Sf])
        P.op("dve", lambda e: e.memset(Sb[:, :, :], 0.0), writes=[t_Sb])

        conv = []
        for i in range(4):
            conv.append((Dm["w_out_b"][512 * i:512 * i + 512, :], Dm["w_out"][512 * i:512 * i + 512, :]))
        for i in range(16):
            conv.append((Dm["w_up_b"][128 * i:128 * i + 128, :], Dm["w_up"][128 * i:128 * i + 128, :]))
        for i in range(16):
            conv.append((Dm["w_down_b"][512 * i:512 * i + 512, :], Dm["w_down"][512 * i:512 * i + 512, :]))

        xo = kview(Dm["xT_oth"])
        xw = kview(Dm["xT_own"])
        steps = []
        for tl in range(24):
            for cc in range(4):
                steps.append(dict(src=xo, col=tl * 512, cc=cc, own=False, ts=tl * 4 + cc, first=(cc == 0)))
        for tau in range(7, -1, -1):
            for cc in range(3, -1, -1):
                steps.append(dict(src=xw, col=(tau + 2) * 512, cc=cc, own=True, ts=None, first=(cc == 3), tau=tau))
        nst = len(steps)
        tile_no = [0]

        def emit_proj(n):
            s = steps[n]
            if s["first"]:
                tile_no[0] += 1
            xs = tile_no[0] % 2
            s["xs"] = xs
            if s["first"]:
                P.op("pool", lambda e: e.dma_start(out=xt[xs][:, :, :], in_=s["src"][:, :, s["col"]:s["col"] + 512]),
                     writes=[t_x[xs]], dsem=f"x{xs}")
                if conv:
                    dst, srcw = conv.pop(0)
                    P.op("pool", lambda e: e.dma_start(out=dst, in_=srcw), dsem="conv")
            b = n % 2
            c0 = s["cc"] * 128
            for half in range(2):
                for kc in range(16):
                    P.op("pe", lambda e, kc=kc, half=half: e.matmul(
                        kp[:, half * 512:half * 512 + 512], xt[xs][:, kc, c0:c0 + 128],
                        wkv[:, kc, half * 512:half * 512 + 512], start=(kc == 0), stop=(kc == 15)),
                        reads=[t_x[xs], t_w[kc // 4]], writes=[t_kp])
            P.op("act", lambda e: e.activation(out=ksb[b][:, :], in_=kp[:, :], func=AF.Copy),
                 reads=[t_kp], writes=[t_ks[b]])
            for half in range(2):
                for kc in range(16):
                    P.op("pe", lambda e, kc=kc, half=half: e.matmul(
                        vp[:, half * 512:half * 512 + 512], xt[xs][:, kc, c0:c0 + 128],
                        wkv[:, kc, 1024 + half * 512:1024 + half * 512 + 512], start=(kc == 0), stop=(kc == 15)),
                        reads=[t_x[xs], t_w[kc // 4]], writes=[t_vp])
            vp3 = vp[:, :].rearrange("p (h e) -> p h e", e=128)
            if s["own"]:
                ci = s["tau"] * 4 + s["cc"]
                P.op("dve", lambda e: e.tensor_tensor(out=vb[b][:, :, :], in0=vp3, in1=bc(otab[:, 8:16], [128, 8, 128]),
                                                      op=ALU.mult), reads=[t_vp, t_tab], writes=[t_vb[b]])
                P.op("dve", lambda e: e.tensor_tensor(out=vf[b][:, :, :], in0=vp3, in1=bc(otab[:, 0:8], [128, 8, 128]),
                                                      op=ALU.mult), reads=[t_vp, t_tab], writes=[t_vf[b]])
                P.op("sp", lambda e: e.dma_start(out=Dm["own_k"][ci], in_=ksb[b][:, :]), reads=[t_ks[b]], dsem=f"ks{b}")
                P.op("sp", lambda e: e.dma_start(out=Dm["own_vf"][ci], in_=vf[b][:, :, :].rearrange("p h e -> p (h e)")),
                     reads=[t_vf[b]], dsem=f"vfs{b}")
                P.op("sp", lambda e: e.dma_start(out=Dm["own_vb"][ci], in_=vb[b][:, :, :].rearrange("p h e -> p (h e)")),
                     reads=[t_vb[b]], dsem=f"vbs{b}")
            else:
                ts = s["ts"]
                P.op("dve", lambda e: e.tensor_tensor(out=vf[b][:, :, :], in0=vp3,
                                                      in1=bc(rs_oth[:, ts, 0:8], [128, 8, 128]), op=ALU.mult),
                     reads=[t_vp, t_tab], writes=[t_vf[b]])
                P.op("dve", lambda e: e.tensor_tensor(out=vb[b][:, :, :], in0=vp3,
                                                      in1=bc(rs_oth[:, ts, 8:16], [128, 8, 128]), op=ALU.mult),
                     reads=[t_vp, t_tab], writes=[t_vb[b]])

        def emit_kv(n):
            s = steps[n]
            b = n % 2
            if not s["own"]:
                ts = s["ts"]
                for h in range(8):
                    P.op("pe", lambda e, h=h: e.matmul(kvf[:, h, :], ksb[b][:, h * 128:h * 128 + 128], vf[b][:, h, :],
                                                      start=True, stop=True),
                         reads=[t_ks[b], t_vf[b]], writes=[t_kvf])
                P.op("dve", lambda e: e.tensor_tensor(out=Sf[:, :, :], in0=Sf[:, :, :],
                                                      in1=bc(af_oth[:, ts, :], [128, 8, 128]), op=ALU.mult),
                     reads=[t_tab], writes=[t_Sf])
                P.op("dve", lambda e: e.tensor_tensor(out=Sf[:, :, :], in0=Sf[:, :, :], in1=kvf[:, :, :], op=ALU.add),
                     reads=[t_kvf], writes=[t_Sf])
            for h in range(8):
                P.op("pe", lambda e, h=h: e.matmul(kvb[:, h, :], ksb[b][:, h * 128:h * 128 + 128], vb[b][:, h, :],
                                                  start=True, stop=True),
                     reads=[t_ks[b], t_vb[b]], writes=[t_kvb])
            P.op("dve", lambda e: e.tensor_tensor(out=Sb[:, :, :], in0=Sb[:, :, :],
                                                  in1=bc(otab[:, 40:48], [128, 8, 128]), op=ALU.mult),
                 reads=[t_tab], writes=[t_Sb])
            P.op("dve", lambda e: e.tensor_tensor(out=Sb[:, :, :], in0=Sb[:, :, :], in1=kvb[:, :, :], op=ALU.add),
                 reads=[t_kvb], writes=[t_Sb])
            if n + 1 < nst and steps[n + 1]["own"] and steps[n + 1]["first"]:
                tau = steps[n + 1]["tau"]
                P.op("sp", lambda e: e.dma_start(out=Dm["sb_bound"][tau], in_=Sb[:, :, :]), reads=[t_Sb], dsem="sbst")
                if not s["own"]:
                    P.op("sp", lambda e: e.dma_start(out=Dm["sf_in"], in_=Sf[:, :, :]), reads=[t_Sf], dsem="sfst")

        for n in range(nst):
            emit_proj(n)
            if n > 0:
                emit_kv(n - 1)
        emit_kv(nst - 1)
        while conv:
            dst, srcw = conv.pop(0)
            P.op("pool", lambda e, dst=dst, srcw=srcw: e.dma_start(out=dst, in_=srcw), dsem="conv")
        P.emit()


def phase1r(nc, Dm):
    with contextlib.ExitStack() as st:
        P = Prog(nc)
        T = lambda n, s, d: st.enter_context(nc.sbuf_tensor(n, s, d))
        PS = lambda n, s, d=F32: st.enter_context(nc.psum_tensor(n, s, d))
        wr = T("r_w", [128, 16, 1024], BF16)
        xt = [T(f"r_xt{i}", [128, 16, 512], BF16) for i in range(2)]
        qT = [T(f"r_qT{i}", [128, 4, 512], BF16) for i in range(2)]
        kT = [T(f"r_kT{i}", [128, 4, 512], BF16) for i in range(2)]
        ktok = [T(f"r_kt{i}", [128, 4, 512], BF16) for i in range(2)]
        vf = [T(f"r_vf{i}", [128, 4, 4, 128], BF16) for i in range(2)]
        vb = [T(f"r_vb{i}", [128, 4, 4, 128], BF16) for i in range(2)]
        sg = [T(f"r_sg{i}", [128, 4, 512], F32) for i in range(2)]
        S = [T("r_Sf", [128, 4, 128], F32), T("r_Sb0", [128, 4, 128], F32), T("r_Sb1", [128, 4, 128], F32)]
        W = [T("r_Wf", [128, 4, 128], BF16), T("r_Wb", [128, 4, 128], BF16)]
        Pm = [T(f"r_P{i}", [128, 4, 128], BF16) for i in range(2)]
        r = T("r_r", [128, 16, 128], F32)
        tmp = T("r_tmp", [128, 16, 128], F32)
        sq = T("r_sq", [128, 16, 128], F32)
        rtk = T("r_rtk", [128, 4, 512], BF16)
        retT = [T(f"r_retT{i}", [128, 4, 512], BF16) for i in range(2)]
        otab = T("r_ot", [128, 48], F32)
        msk = T("r_msk", [128, 2, 4, 128], F32)
        gng = T("r_gng", [128, 1024], F32)
        stt = T("r_st", [128, 6, 16], F32)
        ident, t_id = mk_ident(P, nc, st, "r_id")
        pq = [PS(f"r_pq{i}", [128, 512]) for i in range(2)]
        sc = [PS(f"r_sc{i}", [128, 4, 128]) for i in range(2)]
        ob = [PS(f"r_o{i}", [128, 4, 128]) for i in range(2)]
        kv = PS("r_kv", [128, 4, 128])
        tr = PS("r_tr", [128, 4, 128], BF16)
        t_w = [Tok() for _ in range(2)]
        t_x = [Tok(), Tok()]
        t_q, t_k, t_sg = [Tok(), Tok()], [Tok(), Tok()], [Tok(), Tok()]
        t_kt, t_vf, t_vb = [Tok(), Tok()], [Tok(), Tok()], [Tok(), Tok()]
        t_S = [Tok(), Tok(), Tok()]
        t_W = [Tok(), Tok()]
        t_P = [Tok(), Tok()]
        t_r = [Tok() for _ in range(4)]
        t_tmp = [Tok() for _ in range(4)]
        t_sq, t_rtk, t_st, t_tab = Tok(), Tok(), Tok(), Tok()
        t_rT = [Tok(), Tok()]
        t_pq = [Tok() for _ in range(2)]
        t_sc, t_o, t_kv, t_tr = [Tok(), Tok()], [Tok(), Tok()], Tok(), Tok()
        P.op("sp", lambda e: e.dma_start(out=otab[:, :], in_=Dm["own_tab"]), writes=[t_tab], dsem="tab")
        P.op("sp", lambda e: e.dma_start(out=msk[:, :, :, :], in_=Dm["rmask"]), writes=[t_tab], dsem="tab")
        P.op("sp", lambda e: e.dma_start(out=gng[:, :], in_=Dm["gng"]), writes=[t_tab], dsem="tab")
        win = kview(Dm["w_in"])
        xw = kview(Dm["xT_own"])
        mixv = kview(Dm["mixT"])
        pqi = [0]
        pi = [0]
        tp = [0]
        prev = [None]
        L = (wr, xt, qT, kT, ktok, vf, vb, sg, S, W, Pm, r, tmp, sq, rtk, retT, otab, msk, gng, stt, ident, pq, sc, ob, kv, tr,
             t_w, t_x, t_q, t_k, t_kt, t_vf, t_vb, t_sg, t_S, t_W, t_P, t_r, t_tmp, t_sq, t_rtk, t_st, t_tab, t_rT, t_pq,
             t_sc, t_o, t_kv, t_tr, t_id, xw, mixv, pqi, pi)
        loads, units, front_b, back_gn, back_tr = _phase1r_tile_factory(P, nc, Dm, L)

        for hg in range(2):
            for wi, seg in enumerate((0, 3)):
                c0 = seg * 1024 + hg * 512
                P.op("pool", lambda e, wi=wi, c0=c0: e.dma_start(out=wr[:, :, wi * 512:wi * 512 + 512],
                                                               in_=win[:, :, c0:c0 + 512]),
                     writes=[t_w[wi]], dsem=f"w{wi}")
            loads(hg, 0, tp[0] % 2)
            for u_ in units(hg, 0, tp[0] % 2):
                u_()
            for j in range(8):
                par = tp[0] % 2
                extra = []
                if j + 1 < 8:
                    loads(hg, j + 1, 1 - par)
                    extra = units(hg, j + 1, 1 - par)
                mid = None
                if prev[0] is not None:
                    pv = prev[0]
                    mid = lambda pv=pv: back_tr(*pv)
                front_b(hg, j, par, extra, mid)
                back_gn(hg, j, par)
                prev[0] = (hg, j, par)
                tp[0] += 1
        back_tr(*prev[0])
        P.emit()


def _phase1r_tile_factory(P, nc, Dm, L):
    (wr, xt, qT, kT, ktok, vf, vb, sg, S, W, Pm, r, tmp, sq, rtk, retT, otab, msk, gng, stt, ident, pq, sc, ob, kv, tr,
     t_w, t_x, t_q, t_k, t_kt, t_vf, t_vb, t_sg, t_S, t_W, t_P, t_r, t_tmp, t_sq, t_rtk, t_st, t_tab, t_rT, t_pq,
     t_sc, t_o, t_kv, t_tr, t_id, xw, mixv, pqi, pi) = L

    def proj(lhsT_fn, rhs_fn, reads, evac):
        i = pqi[0] % 2
        pqi[0] += 1
        for kc in range(16):
            P.op("pe", lambda e, kc=kc: e.matmul(pq[i][:, :], lhsT_fn(kc), rhs_fn(kc), start=(kc == 0), stop=(kc == 15)),
                 reads=reads, writes=[t_pq[i]])
        evac(pq[i], t_pq[i])

    def loads(hg, j, xs):
                P.op("pool", lambda e: e.dma_start(out=xt[xs][:, :, :], in_=xw[:, :, (j + 2) * 512:(j + 3) * 512]),
                     writes=[t_x[xs]], dsem=f"x{xs}")
                if j == 0:
                    P.op("sp", lambda e: e.dma_start(out=S[0][:, :, :], in_=Dm["sf_in"][:, hg * 4:hg * 4 + 4, :]),
                         writes=[t_S[0]], dsem="sf")
                P.op("sp", lambda e: e.dma_start(out=S[1 + xs][:, :, :], in_=Dm["sb_bound"][j][:, hg * 4:hg * 4 + 4, :]),
                     writes=[t_S[1 + xs]], dsem=f"sb{xs}")
                for nm, dst, tk in (("own_k", ktok[xs][:, :, :], t_kt[xs]),
                                    ("own_vf", vf[xs][:, :, :, :].rearrange("p c h e -> p c (h e)"), t_vf[xs]),
                                    ("own_vb", vb[xs][:, :, :, :].rearrange("p c h e -> p c (h e)"), t_vb[xs])):
                    src = Dm[nm][j * 4:(j + 1) * 4].rearrange("c p n -> p c n")[:, :, hg * 512:(hg + 1) * 512]
                    P.op("sp", lambda e, dst=dst, src=src: e.dma_start(out=dst, in_=src), writes=[tk], dsem=f"{nm}{xs}")

    def units(hg, j, xs):
                us = []
                for h in range(4):
                    us.append(lambda h=h: proj(
                        lambda kc: wr[:, kc, h * 128:h * 128 + 128], lambda kc: xt[xs][:, kc, :], [t_w[0], t_x[xs]],
                        lambda ps, tps: P.op("act", lambda e: e.activation(out=qT[xs][:, h, :], in_=ps[:, :], func=AF.Copy),
                                             reads=[tps], writes=[t_q[xs]])))

                def kt_unit(h):
                    for cc in range(4):
                        P.op("pe", lambda e, cc=cc: e.transpose(tr[:, cc, :], ktok[xs][:, cc, h * 128:h * 128 + 128], ident[:, :]),
                             reads=[t_kt[xs], t_id], writes=[t_tr])
                    P.op("act", lambda e: e.activation(out=kT[xs][:, h, :].rearrange("p (c t) -> p c t", t=128),
                                                       in_=tr[:, :, :], func=AF.Copy),
                         reads=[t_tr], writes=[t_k[xs]])
                for h in range(4):
                    us.append(lambda h=h: kt_unit(h))
                for cc in range(4):
                    us.append(lambda cc=cc: proj(
                        lambda kc: xt[xs][:, kc, cc * 128:cc * 128 + 128], lambda kc: wr[:, kc, 512:1024], [t_w[1], t_x[xs]],
                        lambda ps, tps: P.op("act", lambda e: e.activation(out=sg[xs][:, cc, :], in_=ps[:, :], func=AF.Silu),
                                             reads=[tps], writes=[t_sg[xs]])))
                return us

    def front_b(hg, j, par, extra, mid):
                r_done = [False] * 4
                kt_, tkt_ = ktok[par], t_kt[par]
                qT_, kT_, tq_, tk_ = qT[par], kT[par], t_q[par], t_k[par]
                Sd = [S[0], S[1 + par]]
                tSd = [t_S[0], t_S[1 + par]]
                nstep = [0]
                pending = []

                def chunk(cc, d):
                    vt, tvt = (vf[par], t_vf[par]) if d == 0 else (vb[par], t_vb[par])
                    gcol = 32 + 8 * d + hg * 4
                    ocol = 16 + 8 * d + hg * 4
                    p_i = pi[0] % 2
                    pi[0] += 1
                    P.op("dve", lambda e: e.tensor_tensor(out=W[d][:, :, :], in0=Sd[d][:, :, :],
                                                          in1=bc(otab[:, gcol:gcol + 4], [128, 4, 128]), op=ALU.mult),
                         reads=[tSd[d], t_tab], writes=[t_W[d]])
                    for h in range(4):
                        P.op("pe", lambda e, h=h: e.matmul(sc[d][:, h, :], kT_[:, h, cc * 128:cc * 128 + 128],
                                                          qT_[:, h, cc * 128:cc * 128 + 128], start=True, stop=True),
                             reads=[tk_, tq_], writes=[t_sc[d]])
                    P.op("dve", lambda e: e.tensor_tensor(out=Pm[p_i][:, :, :], in0=sc[d][:, :, :], in1=msk[:, d, :, :], op=ALU.mult),
                         reads=[t_sc[d], t_tab], writes=[t_P[p_i]])
                    for h in range(4):
                        P.op("pe", lambda e, h=h: e.matmul(kv[:, h, :], kt_[:, cc, h * 128:h * 128 + 128], vt[:, cc, h, :],
                                                          start=True, stop=True),
                             reads=[tkt_, tvt], writes=[t_kv])
                    P.op("dve", lambda e: e.tensor_tensor(out=Sd[d][:, :, :], in0=Sd[d][:, :, :],
                                                          in1=bc(otab[:, gcol:gcol + 4], [128, 4, 128]), op=ALU.mult),
                         reads=[t_tab], writes=[tSd[d]])
                    P.op("dve", lambda e: e.tensor_tensor(out=Sd[d][:, :, :], in0=Sd[d][:, :, :], in1=kv[:, :, :], op=ALU.add),
                         reads=[t_kv], writes=[tSd[d]])
                    if pending:
                        pending.pop(0)()
                    for _ in range(2 if nstep[0] < 4 else 1):
                        if extra:
                            extra.pop(0)()
                    if nstep[0] == 6 and mid is not None:
                        mid()
                    nstep[0] += 1
                    for h in range(4):
                        P.op("pe", lambda e, h=h: e.matmul(ob[d][:, h, :], Pm[p_i][:, h, :], vt[:, cc, h, :],
                                                          start=(h == 0), stop=False, skip_group_check=True),
                             reads=[t_P[p_i], tvt], writes=[t_o[d]])
                        P.op("pe", lambda e, h=h: e.matmul(ob[d][:, h, :], qT_[:, h, cc * 128:cc * 128 + 128], W[d][:, h, :],
                                                          start=False, stop=True, skip_group_check=True),
                             reads=[tq_, t_W[d]], writes=[t_o[d]])
                    rv = r[:, cc * 4:cc * 4 + 4, :]
                    first = not r_done[cc]
                    r_done[cc] = True

                    def evac():
                        if first:
                            P.op("dve", lambda e: e.tensor_tensor(out=rv, in0=ob[d][:, :, :], in1=bc(otab[:, ocol:ocol + 4], [128, 4, 128]),
                                                                  op=ALU.mult), reads=[t_o[d], t_tab], writes=[t_r[cc]])
                        else:
                            tv = tmp[:, cc * 4:cc * 4 + 4, :]
                            P.op("dve", lambda e: e.tensor_tensor(out=tv, in0=ob[d][:, :, :], in1=bc(otab[:, ocol:ocol + 4], [128, 4, 128]),
                                                                  op=ALU.mult), reads=[t_o[d], t_tab], writes=[t_tmp[cc]])
                            P.op("pool", lambda e: e.tensor_tensor(out=rv, in0=rv, in1=tv, op=ALU.add),
                                 reads=[t_tmp[cc]], writes=[t_r[cc]])
                    pending.append(evac)

                for cc in range(4):
                    chunk(cc, 0)
                    chunk(3 - cc, 1)
                while pending:
                    pending.pop(0)()
                while extra:
                    extra.pop(0)()
    def back_gn(hg, j, sgi):
                allr = list(t_r)
                P.op("dve", lambda e: e.tensor_reduce(out=stt[:, 0, :], in_=r[:, :, :], axis=AX.X, op=ALU.add),
                     reads=allr, writes=[t_st])
                P.op("dve", lambda e: e.tensor_tensor(out=sq[:, :, :], in0=r[:, :, :], in1=r[:, :, :], op=ALU.mult),
                     reads=allr, writes=[t_sq])
                P.op("dve", lambda e: e.tensor_reduce(out=stt[:, 1, :], in_=sq[:, :, :], axis=AX.X, op=ALU.add),
                     reads=[t_sq], writes=[t_st])
                P.op("dve", lambda e: e.tensor_scalar(out=stt[:, 2, :], in0=stt[:, 0, :], scalar1=1.0 / 128, scalar2=None, op0=ALU.mult),
                     reads=[t_st], writes=[t_st])
                P.op("dve", lambda e: e.tensor_tensor(out=stt[:, 3, :], in0=stt[:, 2, :], in1=stt[:, 2, :], op=ALU.mult),
                     reads=[t_st], writes=[t_st])
                P.op("dve", lambda e: e.scalar_tensor_tensor(out=stt[:, 4, :], in0=stt[:, 1, :], scalar=1.0 / 128, in1=stt[:, 3, :],
                                                             op0=ALU.mult, op1=ALU.subtract), reads=[t_st], writes=[t_st])
                P.op("dve", lambda e: e.tensor_scalar(out=stt[:, 5, :], in0=stt[:, 4, :], scalar1=EPS, scalar2=None,
                                                      op0=ALU.add), reads=[t_st], writes=[t_st])
                P.op("act", lambda e: e.activation(out=stt[:, 4, :], in_=stt[:, 5, :], func=AF.Sqrt), reads=[t_st], writes=[t_st])
                P.op("dve", lambda e: e.reciprocal(out=stt[:, 5, :], in_=stt[:, 4, :]), reads=[t_st], writes=[t_st])
                alltmp = list(t_tmp)
                P.op("dve", lambda e: e.tensor_tensor(out=tmp[:, :, :], in0=r[:, :, :], in1=bc(stt[:, 2, :], [128, 16, 128]),
                                                      op=ALU.subtract), reads=allr + [t_st], writes=alltmp)
                P.op("dve", lambda e: e.tensor_tensor(out=tmp[:, :, :], in0=tmp[:, :, :], in1=bc(stt[:, 5, :], [128, 16, 128]),
                                                       op=ALU.mult), reads=[t_st], writes=alltmp)
                tmp4 = tmp[:, :, :].rearrange("p (c h) e -> p c (h e)", h=4)
                P.op("dve", lambda e, hg=hg: e.tensor_tensor(
                    out=tmp4, in0=tmp4, in1=gng[:, hg * 512:hg * 512 + 512].unsqueeze(1).to_broadcast([128, 4, 512]),
                    op=ALU.mult), reads=[t_tab], writes=alltmp)
                P.op("dve", lambda e: e.tensor_tensor(out=rtk[:, :, :], in0=tmp4, in1=sg[sgi][:, :, :], op=ALU.mult),
                     reads=alltmp + [t_sg[sgi]], writes=[t_rtk])
    def back_tr(hg, j, rs_):
                for h in range(4):
                    for cc in range(4):
                        P.op("pe", lambda e, h=h, cc=cc: e.transpose(tr[:, cc, :], rtk[:, cc, h * 128:h * 128 + 128], ident[:, :]),
                             reads=[t_rtk, t_id], writes=[t_tr])
                    P.op("act", lambda e, h=h: e.activation(out=retT[rs_][:, h, :].rearrange("p (c t) -> p c t", t=128),
                                                           in_=tr[:, :, :], func=AF.Copy),
                         reads=[t_tr], writes=[t_rT[rs_]])
                P.op("sp", lambda e: e.dma_start(out=mixv[:, hg * 4:hg * 4 + 4, j * 512:j * 512 + 512],
                                                 in_=retT[rs_][:, :, :]),
                     reads=[t_rT[rs_]], dsem=f"mx{rs_}")

    return loads, units, front_b, back_gn, back_tr


def phase1a(nc, Dm):
    with contextlib.ExitStack() as st:
        P = Prog(nc)
        T = lambda n, s, d: st.enter_context(nc.sbuf_tensor(n, s, d))
        PS = lambda n, s, d=F32: st.enter_context(nc.psum_tensor(n, s, d))
        wa = T("a_w", [128, 16, 1536], BF16)
        xt = [T(f"a_xt{i}", [128, 16, 512], BF16) for i in range(2)]
        kaT = T("a_kT", [128, 6, 4, 512], BF16)
        va = T("a_v", [128, 24, 4, 130], BF16)
        qaT = T("a_qT", [128, 4, 4, 512], BF16)
        am = T("a_m", [128, 4, 17, 128], BF16)
        E = [T(f"a_E{i}", [128, 512], F32) for i in range(4)]
        Pb = [T(f"a_P{i}", [128, 512], BF16) for i in range(4)]
        otk = [T(f"a_otk{i}", [128, 4, 512], BF16) for i in range(2)]
        attT = [T(f"a_aT{i}", [128, 4, 512], BF16) for i in range(2)]
        valid = T("a_val", [128, 48], F32)
        rec = T("a_rec", [128, 4], F32)
        ident, t_id = mk_ident(P, nc, st, "a_id")
        pq = [PS(f"a_pq{i}", [128, 512]) for i in range(2)]
        scp = [PS(f"a_sc{i}", [128, 512]) for i in range(3)]
        accb = [PS(f"a_acc{i}", [128, 512]) for i in range(2)]
        acc = [accb[t // 2][:, (t % 2) * 256:(t % 2) * 256 + 129] for t in range(4)]
        tr = PS("a_tr", [128, 4, 128], BF16)
        t_w = [Tok() for _ in range(3)]
        t_x = [Tok(), Tok()]
        t_kT = [Tok() for _ in range(6)]
        t_v = [Tok() for _ in range(6)]
        t_qT = [Tok() for _ in range(4)]
        t_am, t_val = Tok(), Tok()
        t_E = [Tok() for _ in range(4)]
        t_Pb = [Tok() for _ in range(4)]
        t_otk, t_rec = [Tok(), Tok()], Tok()
        t_aT = [Tok(), Tok()]
        t_pq, t_tr = [Tok(), Tok()], Tok()
        t_sc = [Tok(), Tok(), Tok()]
        t_accb = [Tok(), Tok()]
        t_acc = [t_accb[t // 2] for t in range(4)]
        pqi = [0]
        pend = [None]
        P.op("sp", lambda e: e.dma_start(out=valid[:, :], in_=Dm["valid"]), writes=[t_val], dsem="val")
        win = kview(Dm["w_in"])
        xw = kview(Dm["xT_own"])
        mixv = kview(Dm["mixT"])
        am_src = Dm["amask"].rearrange("p (h o q) -> p h o q", h=8, o=17)
        tp = [0]
        rr = [0]
        ao = [0]
        def proj(lhsT_fn, rhs_fn, reads, evac):
            pi_ = pqi[0] % 2
            pqi[0] += 1
            for kc in range(16):
                P.op("pe", lambda e, kc=kc: e.matmul(pq[pi_][:, :], lhsT_fn(kc), rhs_fn(kc), start=(kc == 0), stop=(kc == 15)),
                     reads=reads, writes=[t_pq[pi_]])
            evac(pq[pi_], t_pq[pi_])

        def loads(hg, i, xs):
            P.op("pool", lambda e: e.dma_start(out=xt[xs][:, :, :], in_=xw[:, :, i * 512:(i + 1) * 512]),
                 writes=[t_x[xs]], dsem=f"x{xs}")

        def units(hg, i, xs):
            sl = i % 6
            us = []
            for h in range(4):
                us.append(lambda h=h: proj(
                    lambda kc: wa[:, kc, 512 + h * 128:512 + h * 128 + 128], lambda kc: xt[xs][:, kc, :], [t_w[1], t_x[xs]],
                    lambda pp, tpp: P.op("act", lambda e: e.activation(out=kaT[:, sl, h, :], in_=pp[:, :], func=AF.Copy),
                                         reads=[tpp], writes=[t_kT[sl]])))

            def v_unit(cc):
                blk = sl * 4 + cc
                proj(lambda kc: xt[xs][:, kc, cc * 128:cc * 128 + 128], lambda kc: wa[:, kc, 1024:1536], [t_w[2], t_x[xs]],
                     lambda pp, tpp: P.op("dve", lambda e: e.tensor_copy(va[:, blk, :, 0:128],
                                                                        pp[:, :].rearrange("p (h e) -> p h e", e=128)),
                                          reads=[tpp], writes=[t_v[sl]]))
                P.op("pool", lambda e: e.tensor_copy(
                    va[:, blk, :, 128:129], valid[:, i * 4 + cc:i * 4 + cc + 1].unsqueeze(1).to_broadcast([128, 4, 1])),
                    reads=[t_val], writes=[t_v[sl]])
            for cc in range(4):
                us.append(lambda cc=cc: v_unit(cc))
            if 2 <= i <= 9:
                qsw = i % 4
                for h in range(4):
                    us.append(lambda h=h: proj(
                        lambda kc: wa[:, kc, h * 128:h * 128 + 128], lambda kc: xt[xs][:, kc, :], [t_w[0], t_x[xs]],
                        lambda pp, tpp: P.op("act", lambda e: e.activation(out=qaT[:, qsw, h, :], in_=pp[:, :], func=AF.Copy),
                                             reads=[tpp], writes=[t_qT[qsw]])))
            return us

        def att(hg, i, oi, extra, mid):
                qs = (i - 2) % 4
                OMAX = (1, 2, 4, 8, 8, 8, 8, 8)
                items = []
                for h in range(4):
                    om = OMAX[hg * 4 + h]
                    for rk in range(20):
                        t_lo, t_hi = max(0, rk - 8 - om), min(3, rk - 8 + om)
                        if t_lo > t_hi:
                            continue
                        items.append((h, rk, t_lo, t_hi, om))

                def stage_a(n):
                    h, rk, t_lo, t_hi, om = items[n]
                    ncol = (t_hi - t_lo + 1) * 128
                    ksl = (i - 4 + rk // 4) % 6
                    kc4 = rk % 4
                    b2 = n % 3
                    b3 = n % 4
                    q0 = t_lo * 128
                    P.op("pe", lambda e: e.matmul(
                        scp[b2][:, 0:ncol], kaT[:, ksl, h, kc4 * 128:kc4 * 128 + 128], qaT[:, qs, h, q0:q0 + ncol],
                        start=True, stop=True), reads=[t_kT[ksl], t_qT[qs]], writes=[t_sc[b2]])
                    P.op("act", lambda e: e.activation(out=E[b3][:, 0:ncol], in_=scp[b2][:, 0:ncol], func=AF.Exp, scale=SC),
                         reads=[t_sc[b2]], writes=[t_E[b3]])
                    o_lo = 16 + t_lo - rk
                    nt = t_hi - t_lo + 1
                    P.op("dve", lambda e: e.tensor_tensor(
                        out=Pb[b3][:, 0:ncol].rearrange("p (o q) -> p o q", q=128),
                        in0=E[b3][:, 0:ncol].rearrange("p (o q) -> p o q", q=128),
                        in1=am[:, h, o_lo:o_lo + nt, :], op=ALU.mult),
                        reads=[t_E[b3], t_am], writes=[t_Pb[b3]])

                def stage_b(n):
                    h, rk, t_lo, t_hi, om = items[n]
                    ksl = (i - 4 + rk // 4) % 6
                    blk = ksl * 4 + rk % 4
                    b3 = n % 4
                    for t in range(t_lo, t_hi + 1):
                        P.op("pe", lambda e, t=t: e.matmul(
                            acc[t], Pb[b3][:, (t - t_lo) * 128:(t - t_lo) * 128 + 128], va[:, blk, h, 0:129],
                            start=(t % 2 == 0 and rk == 8 + t - om), stop=(rk == 8 + t + om), skip_group_check=True),
                            reads=[t_Pb[b3], t_v[ksl]], writes=[t_acc[t]])
                    if n + 1 == len(items) or items[n + 1][0] != h:
                        for t in range(4):
                            P.op("dve", lambda e, t=t: e.reciprocal(out=rec[:, t:t + 1], in_=acc[t][:, 128:129]),
                                 reads=[t_acc[t]], writes=[t_rec])
                            P.op("dve", lambda e, t=t: e.tensor_scalar(out=otk[oi][:, t, h * 128:h * 128 + 128], in0=acc[t][:, 0:128],
                                                                      scalar1=rec[:, t:t + 1], scalar2=None, op0=ALU.mult),
                                 reads=[t_acc[t], t_rec], writes=[t_otk[oi]])

                SK = 2
                every = max(1, len(items) // (len(extra) + 1)) if extra else 0
                for n in range(len(items) + SK):
                    if n < len(items):
                        stage_a(n)
                    if n == 3 and mid is not None:
                        mid()
                    if extra and n > 0 and n % every == 0:
                        extra.pop(0)()
                    if n >= SK:
                        stage_b(n - SK)
                while extra:
                    extra.pop(0)()

        def back(hg, j, oi):
                a_ = ao[0] % 2
                ao[0] += 1
                for h in range(4):
                    for t in range(4):
                        P.op("pe", lambda e, h=h, t=t: e.transpose(tr[:, t, :], otk[oi][:, t, h * 128:h * 128 + 128], ident[:, :]),
                             reads=[t_otk[oi], t_id], writes=[t_tr])
                    P.op("act", lambda e, h=h, a_=a_: e.activation(out=attT[a_][:, h, :].rearrange("p (c t) -> p c t", t=128),
                                                                  in_=tr[:, :, :], func=AF.Copy),
                         reads=[t_tr], writes=[t_aT[a_]])
                P.op("sp", lambda e: e.dma_start(out=mixv[:, 8 + hg * 4:8 + hg * 4 + 4, j * 512:j * 512 + 512],
                                                 in_=attT[a_][:, :, :]),
                     reads=[t_aT[a_]], dsem=f"mx{a_}")

        for hg in range(2):
            for seg in range(3):
                c0 = 4096 + seg * 1024 + hg * 512
                P.op("pool", lambda e, seg=seg, c0=c0: e.dma_start(out=wa[:, :, seg * 512:seg * 512 + 512],
                                                                 in_=win[:, :, c0:c0 + 512]),
                     writes=[t_w[seg]], dsem=f"w{seg}")
            P.op("pool", lambda e, hg=hg: e.dma_start(out=am[:, :, :, :], in_=am_src[:, hg * 4:hg * 4 + 4, :, :]),
                 writes=[t_am], dsem="am")
            loads(hg, 0, tp[0] % 2)
            for u_ in units(hg, 0, tp[0] % 2):
                u_()
            for i in range(12):
                xs = tp[0] % 2
                tp[0] += 1
                extra = []
                if i + 1 < 12:
                    loads(hg, i + 1, 1 - xs)
                    extra = units(hg, i + 1, 1 - xs)
                if i >= 4:
                    mid = None
                    if pend[0] is not None:
                        pv = pend[0]
                        mid = lambda pv=pv: back(*pv)
                        pend[0] = None
                    oi = i % 2
                    att(hg, i, oi, extra, mid)
                    pend[0] = (hg, i - 4, oi)
                else:
                    for u_ in extra:
                        u_()
        back(*pend[0])
        P.emit()


def phase2(nc, Dm):
    with contextlib.ExitStack() as st:
        P = Prog(nc)
        T = lambda n, s, d: st.enter_context(nc.sbuf_tensor(n, s, d))
        PS = lambda n, s, d=F32: st.enter_context(nc.psum_tensor(n, s, d))
        y = T("f_y", [128, 8, 2048], F32)
        bufs = [T("f_A", [128, 16, 1024], BF16), T("f_B", [128, 16, 1024], BF16)]
        slab = [T(f"f_sl{i}", [128, 16, 512], BF16) for i in range(3)]
        lng = T("f_lng", [128, 2048], F32)
        lnb = T("f_lnb", [128, 2048], F32)
        hb = T("f_hb", [128, 2048], BF16)
        rl = [T(f"f_rl{i}", [128, 512], F32) for i in range(3)]
        bst = T("f_bst", [128, 8, 24], F32)
        mv = T("f_mv", [128, 8, 4], F32)
        ident, t_id = mk_ident(P, nc, st, "f_id")
        ps = [PS(f"f_ps{i}", [128, 512]) for i in range(6)]
        tr = [PS(f"f_tr{i}", [128, 4, 128], BF16) for i in range(2)]
        t_y = [[Tok() for _ in range(4)] for _ in range(8)]
        t_buf = [[[Tok(), Tok()] for _ in range(16)] for _ in range(2)]
        t_sl = [Tok() for _ in range(3)]
        t_ln, t_hb = Tok(), Tok()
        t_mv = [Tok() for _ in range(8)]
        t_rl = [Tok(), Tok(), Tok()]
        t_ps = [Tok() for _ in range(6)]
        t_tr = [Tok(), Tok()]
        mixv = kview(Dm["mixT"])
        si = [0]
        pi = [0]
        ri = [0]
        ti = [0]

        def load_slab(src):
            s = si[0] % 3
            si[0] += 1
            P.op("sp", lambda e: e.dma_start(out=slab[s][:, :, :], in_=src), writes=[t_sl[s]], dsem=f"sl{s}")
            return s

        def ln_tables(gname, bname):
            P.op("sp", lambda e: e.dma_start(out=lng[:, :], in_=Dm[gname]), writes=[t_ln], dsem="ln")
            P.op("sp", lambda e: e.dma_start(out=lnb[:, :], in_=Dm[bname]), writes=[t_ln], dsem="ln")

        def ln_stats(tb):
            tm = t_mv[tb]
            for c in range(4):
                P.op("dve", lambda e, c=c: e.bn_stats(out=bst[:, tb, c * 6:c * 6 + 6], in_=y[:, tb, c * 512:c * 512 + 512]),
                     reads=[t_y[tb][c]], writes=[tm])
            P.op("dve", lambda e: e.bn_aggr(out=mv[:, tb, 0:2], in_=bst[:, tb, :]), reads=[tm], writes=[tm])
            P.op("dve", lambda e: e.tensor_scalar(out=mv[:, tb, 2:3], in0=mv[:, tb, 1:2], scalar1=EPS, scalar2=None, op0=ALU.add),
                 reads=[tm], writes=[tm])
            P.op("act", lambda e: e.activation(out=mv[:, tb, 3:4], in_=mv[:, tb, 2:3], func=AF.Sqrt), reads=[tm], writes=[tm])
            P.op("dve", lambda e: e.reciprocal(out=mv[:, tb, 2:3], in_=mv[:, tb, 3:4]), reads=[tm], writes=[tm])
            P.op("dve", lambda e: e.scalar_tensor_tensor(out=mv[:, tb, 3:4], in0=mv[:, tb, 0:1], scalar=-1.0, in1=mv[:, tb, 2:3],
                                                         op0=ALU.mult, op1=ALU.mult), reads=[tm], writes=[tm])

        def ln_norm(tb):
            tm = t_mv[tb]
            P.op("act", lambda e: e.activation(out=y[:, tb, :], in_=y[:, tb, :], func=AF.Identity, bias=mv[:, tb, 3:4],
                                               scale=mv[:, tb, 2:3]), reads=[tm], writes=t_y[tb])

        def ln_gb(tb):
            P.op("dve", lambda e: e.tensor_tensor(out=y[:, tb, :], in0=y[:, tb, :], in1=lng[:, :], op=ALU.mult),
                 reads=[t_ln], writes=t_y[tb])
            P.op("dve", lambda e: e.tensor_tensor(out=y[:, tb, :], in0=y[:, tb, :], in1=lnb[:, :], op=ALU.add),
                 reads=[t_ln], writes=t_y[tb])

        def load_mix(tt):
            X = tt % 2
            P.op("pool", lambda e: e.dma_start(out=bufs[X][:, :, :], in_=mixv[:, :, tt * 1024:(tt + 1) * 1024]),
                 writes=[tk for kc in range(16) for tk in t_buf[X][kc]], dsem=f"A{X}")

        def tile(tt):
            X, Yb = tt % 2, 1 - tt % 2
            bX, bY = bufs[X], bufs[Yb]
            tX, tY = t_buf[X], t_buf[Yb]
            if tt == 0:
                load_mix(0)
            for tb in range(8):
                r0 = tt * 1024 + tb * 128
                P.op("pool", lambda e, tb=tb, r0=r0: e.dma_start(out=y[:, tb, :], in_=Dm["x_own"][r0:r0 + 128, :]),
                     writes=t_y[tb], dsem=f"y{tb}")
            for cg in range(4):
                s = load_slab(kview(Dm["w_out_b"])[:, :, cg * 512:cg * 512 + 512])
                for tb in range(8):
                    p = pi[0] % 6
                    pi[0] += 1
                    for kc in range(16):
                        P.op("pe", lambda e, kc=kc, tb=tb, s=s, p=p: e.matmul(
                            ps[p][:, :], bX[:, kc, tb * 128:tb * 128 + 128], slab[s][:, kc, :], start=(kc == 0), stop=(kc == 15)),
                            reads=[tX[kc][tb // 4], t_sl[s]], writes=[t_ps[p]])
                    P.op("dve", lambda e, tb=tb, cg=cg, p=p: e.scalar_tensor_tensor(
                        out=y[:, tb, cg * 512:cg * 512 + 512], in0=y[:, tb, cg * 512:cg * 512 + 512], scalar=ALPHA,
                        in1=ps[p][:, :], op0=ALU.mult, op1=ALU.add), reads=[t_ps[p]], writes=[t_y[tb][cg]])
            ln_tables("ln1g", "ln1b")
            ln_stats(0)
            ln_stats(1)
            ln_norm(0)
            for tb in range(8):
                if tb + 2 < 8:
                    ln_stats(tb + 2)
                if tb + 1 < 8:
                    ln_norm(tb + 1)
                ln_gb(tb)
                P.op("act", lambda e, tb=tb: e.activation(out=hb[:, :], in_=y[:, tb, :], func=AF.Copy),
                     reads=t_y[tb], writes=[t_hb])
                for g4 in range(4):
                    q = ti[0] % 2
                    ti[0] += 1
                    for m in range(4):
                        kc = g4 * 4 + m
                        P.op("pe", lambda e, kc=kc, m=m, q=q: e.transpose(tr[q][:, m, :], hb[:, kc * 128:kc * 128 + 128], ident[:, :]),
                             reads=[t_hb, t_id], writes=[t_tr[q]])
                    P.op("act", lambda e, g4=g4, tb=tb, q=q: e.activation(out=bY[:, g4 * 4:g4 * 4 + 4, tb * 128:tb * 128 + 128],
                                                                         in_=tr[q][:, :, :], func=AF.Copy),
                         reads=[t_tr[q]], writes=[tY[g4 * 4 + m][tb // 4] for m in range(4)])
            for u in range(4):
                for sidx in range(4):
                    c0 = u * 2048 + sidx * 512
                    s = load_slab(kview(Dm["w_up_b"])[:, :, c0:c0 + 512])
                    for m in range(4):
                        for th in range(2):
                            p = pi[0] % 6
                            pi[0] += 1
                            for kc in range(16):
                                P.op("pe", lambda e, kc=kc, m=m, th=th, s=s, p=p: e.matmul(
                                    ps[p][:, :], slab[s][:, kc, m * 128:m * 128 + 128], bY[:, kc, th * 512:th * 512 + 512],
                                    start=(kc == 0), stop=(kc == 15)),
                                    reads=[t_sl[s], tY[kc][th]], writes=[t_ps[p]])
                            q = ri[0] % 3
                            ri[0] += 1
                            fc = sidx * 4 + m
                            P.op("act", lambda e, p=p, q=q: e.activation(out=rl[q][:, :], in_=ps[p][:, :], func=AF.Relu),
                                 reads=[t_ps[p]], writes=[t_rl[q]])
                            P.op("dve", lambda e, q=q, fc=fc, th=th: e.tensor_tensor(
                                out=bX[:, fc, th * 512:th * 512 + 512], in0=rl[q][:, :], in1=rl[q][:, :], op=ALU.mult),
                                reads=[t_rl[q]], writes=[tX[fc][th]])
                if u == 3 and tt + 1 < 4:
                    load_mix(tt + 1)
                for cg in range(4):
                    s = load_slab(kview(Dm["w_down_b"][u * 2048:(u + 1) * 2048, :])[:, :, cg * 512:cg * 512 + 512])
                    for tb in range(8):
                        p = pi[0] % 6
                        pi[0] += 1
                        for kc in range(16):
                            P.op("pe", lambda e, kc=kc, tb=tb, s=s, p=p: e.matmul(
                                ps[p][:, :], bX[:, kc, tb * 128:tb * 128 + 128], slab[s][:, kc, :], start=(kc == 0), stop=(kc == 15)),
                                reads=[tX[kc][tb // 4], t_sl[s]], writes=[t_ps[p]])
                        P.op("dve", lambda e, tb=tb, cg=cg, p=p, u=u: e.scalar_tensor_tensor(
                            out=y[:, tb, cg * 512:cg * 512 + 512], in0=y[:, tb, cg * 512:cg * 512 + 512],
                            scalar=(ALPHA if u == 0 else 1.0), in1=ps[p][:, :], op0=ALU.mult, op1=ALU.add),
                            reads=[t_ps[p]], writes=[t_y[tb][cg]])
            ln_tables("ln2g", "ln2b")
            ln_stats(0)
            ln_stats(1)
            ln_norm(0)
            for tb in range(8):
                if tb + 2 < 8:
                    ln_stats(tb + 2)
                if tb + 1 < 8:
                    ln_norm(tb + 1)
                ln_gb(tb)
                r0 = tt * 1024 + tb * 128
                P.op("sp", lambda e, tb=tb, r0=r0: e.dma_start(out=Dm["out"][r0:r0 + 128, :], in_=y[:, tb, :]),
                     reads=t_y[tb], dsem=f"o{tb}")

        for tt in range(4):
            tile(tt)
        P.emit()


def build_program(debug=False):
    nc = bass.Bass("TRN2", target_bir_lowering=False)
    Dm = {}

    def inp(name, shape):
        Dm[name] = nc.dram_tensor(name, shape, F32, kind="ExternalInput").ap()

    inp("xT_own", [D, 6144])
    inp("xT_oth", [D, 12288])
    inp("x_own", [NTOK, D])
    inp("w_in", [D, 7168])
    inp("w_out", [D, D])
    inp("w_up", [D, 8192])
    inp("w_down", [8192, D])
    inp("rs_oth", [128, 96, 16])
    inp("af_oth", [128, 96, 8])
    inp("own_tab", [128, 48])
    inp("rmask", [128, 2, 4, 128])
    inp("gng", [128, 1024])
    inp("amask", [128, 8 * 17 * 128])
    inp("valid", [128, 48])
    for n in ("ln1g", "ln1b", "ln2g", "ln2b"):
        inp(n, [128, D])
    Dm["out"] = nc.dram_tensor("out", [NTOK, D], F32, kind="ExternalOutput").ap()
    k = "ExternalOutput" if debug else "Internal"
    Dm["mixT"] = nc.dram_tensor("mixT", [D, NTOK], BF16, kind=k).ap()
    Dm["sb_bound"] = nc.dram_tensor("sb_bound", [8, 128, 8, 128], F32, kind=k).ap()
    Dm["sf_in"] = nc.dram_tensor("sf_in", [128, 8, 128], F32, kind=k).ap()
    for n in ("own_k", "own_vf", "own_vb"):
        Dm[n] = nc.dram_tensor(n, [32, 128, 1024], BF16, kind="Internal").ap()
    Dm["w_out_b"] = nc.dram_tensor("w_out_b", [D, D], BF16, kind="Internal").ap()
    Dm["w_up_b"] = nc.dram_tensor("w_up_b", [D, 8192], BF16, kind="Internal").ap()
    Dm["w_down_b"] = nc.dram_tensor("w_down_b", [8192, D], BF16, kind="Internal").ap()
    phase0(nc, Dm)
    phase1r(nc, Dm)
    phase1a(nc, Dm)
    phase2(nc, Dm)
    return nc


def _const_tables(c):
    hh = np.arange(8, dtype=np.float64)
    lgf = np.log1p(-np.exp2(-5.0 - hh))
    lgb = np.log1p(-np.exp2(-5.5 - hh))
    p = np.arange(128, dtype=np.float64)[:, None]
    rsf = SC * np.exp(lgf[None, :] * (127.0 - p))
    rsb = SC * np.exp(lgb[None, :] * p)
    osf = np.exp(lgf[None, :] * (p + 1.0 - 128.0))
    osb = np.exp(-lgb[None, :] * p)
    gf = np.broadcast_to(np.exp(128.0 * lgf)[None, :], (128, 8))
    gb = np.broadcast_to(np.exp(128.0 * lgb)[None, :], (128, 8))
    own_tab = np.concatenate([rsf, rsb, osf, osb, gf, gb], axis=1).astype(np.float32)
    nbefore = c * 32
    rs_oth = np.zeros((128, 96, 16), np.float64)
    af_oth = np.ones((128, 96, 8), np.float64)
    rs_oth[:, :nbefore, 0:8] = rsf[:, None, :]
    rs_oth[:, nbefore:, 8:16] = rsb[:, None, :]
    af_oth[:, :nbefore, :] = gf[:, None, :]
    kk = np.arange(128)[:, None]
    qq = np.arange(128)[None, :]
    mf = (kk <= qq).astype(np.float32)
    mb = (kk > qq).astype(np.float32)
    rmask = np.stack([np.broadcast_to(mf[:, None, :], (128, 4, 128)), np.broadcast_to(mb[:, None, :], (128, 4, 128))], axis=1)
    return own_tab, rs_oth.astype(np.float32), af_oth.astype(np.float32), np.ascontiguousarray(rmask, dtype=np.float32)


def _amask():
    slopes = np.exp2(-8.0 * (np.arange(8, dtype=np.float64) + 1.0) / 8.0)
    k = np.arange(128)[:, None, None]
    o = (np.arange(17) - 8)[None, :, None]
    q = np.arange(128)[None, None, :]
    delta = np.abs(o * 128 + q - k)
    mult = ((delta <= 64).astype(np.float64) + ((delta % 4 == 0) & (delta <= 256)) + ((delta % 16 == 0) & (delta <= 1024)))
    tab = mult[:, None, :, :] * np.exp(-slopes[None, :, None, None] * delta[:, None, :, :])
    return np.ascontiguousarray(tab.reshape(128, 8 * 17 * 128), dtype=np.float32)


_CACHE = {}


def kernel(x, w_in, ret_gn_gain, w_out, ln1_gain, ln1_bias, w_up, w_down, ln2_gain, ln2_bias, _debug=False):
    x = np.asarray(x, np.float32)
    w_in0 = np.ascontiguousarray(np.asarray(w_in, np.float32)[0])
    w_out0 = np.ascontiguousarray(np.asarray(w_out, np.float32)[0])
    w_up0 = np.ascontiguousarray(np.asarray(w_up, np.float32)[0])
    w_down0 = np.ascontiguousarray(np.asarray(w_down, np.float32)[0])
    rep = lambda v, n: np.ascontiguousarray(np.broadcast_to(np.asarray(v, np.float32).reshape(1, n), (128, n)))
    gng = rep(ret_gn_gain, 1024)
    l1g, l1b, l2g, l2b = rep(ln1_gain, D), rep(ln1_bias, D), rep(ln2_gain, D), rep(ln2_bias, D)
    amask = _amask()
    in_maps = []
    for core in range(8):
        b, c = core // 4, core % 4
        t0 = c * NTOK
        xb = x[b]
        xT = np.ascontiguousarray(xb.T)
        xT_own = np.zeros((D, 6144), np.float32)
        lo, hi = t0 - 1024, t0 + NTOK + 1024
        slo, shi = max(lo, 0), min(hi, S_LEN)
        xT_own[:, slo - lo:shi - lo] = xT[:, slo:shi]
        pos = np.arange(lo, hi)
        valid = ((pos >= 0) & (pos < S_LEN)).astype(np.float32).reshape(48, 128).T
        before = xT[:, :t0]
        after = xT[:, t0 + NTOK:]
        na = after.shape[1] // 128
        after_r = after.reshape(D, na, 128)[:, ::-1, :].reshape(D, na * 128)
        xT_oth = np.ascontiguousarray(np.concatenate([before, after_r], axis=1))
        own_tab, rs_oth, af_oth, rmask = _const_tables(c)
        in_maps.append({
            "xT_own": xT_own, "xT_oth": xT_oth, "x_own": np.ascontiguousarray(xb[t0:t0 + NTOK]),
            "w_in": w_in0, "w_out": w_out0, "w_up": w_up0, "w_down": w_down0,
            "rs_oth": rs_oth, "af_oth": af_oth, "own_tab": own_tab, "rmask": rmask, "gng": gng,
            "amask": amask, "valid": np.ascontiguousarray(valid),
            "ln1g": l1g, "ln1b": l1b, "ln2g": l2g, "ln2b": l2b,
        })
    key = bool(_debug)
    if key not in _CACHE:
        _CACHE[key] = build_program(debug=_debug)
    nc = _CACHE[key]
    res = run_bass_kernel_spmd(nc, in_maps, core_ids=list(range(8)))
    out = np.zeros((2, S_LEN, D), np.float32)
    for core in range(8):
        b, c = core // 4, core % 4
        out[b, c * NTOK:(c + 1) * NTOK] = res.results[core]["out"]
    if _debug:
        return out, res
    return out
```
